# Optimizing a Trainium2 kernel written in Bass

```python
import math
import jax, jax.numpy as jnp
from jax import lax
import numpy as np

D_MODEL = 2048
BATCH = 2
SEQ = 4096
DEPTH = 1

CHUNK = 64
QBLOCK = 128
HEAD_DIM = 128
N_HEADS_SB = 8
N_HEADS_DSA = 8
IDX_HEADS = 16
IDX_DIM = 64
TOPK_MAX = 256
N_BUCKETS = 32
MAX_DISTANCE = 128
N_MEM = 256
MEM_HEADS = 4
D_FF = 3 * D_MODEL
CONV_WIDTH = 3
EPS = 1e-6

SB_WIDTH = N_HEADS_SB * HEAD_DIM
DSA_WIDTH = N_HEADS_DSA * HEAD_DIM
MEM_WIDTH = MEM_HEADS * HEAD_DIM
IN_WIDTHS = (SB_WIDTH, SB_WIDTH, SB_WIDTH,
             DSA_WIDTH, DSA_WIDTH, DSA_WIDTH,
             IDX_HEADS * IDX_DIM, IDX_DIM, IDX_HEADS,
             2 * D_MODEL)
IN_TOTAL = sum(IN_WIDTHS)

kernel_name = "hybrid_stickbreak_dsa_convffn_layer"


def rmsnorm(x, g):
    xf = x.astype(jnp.float32)
    inv = lax.rsqrt(jnp.mean(xf * xf, axis=-1, keepdims=True) + EPS)
    return (xf * inv * g.astype(jnp.float32)).astype(x.dtype)


def to_blocks(a, nb):
    b, s = a.shape[:2]
    return a.reshape((b, nb, s // nb) + a.shape[2:]).swapaxes(0, 1)


def from_blocks(a):
    nb, b, q = a.shape[:3]
    return a.swapaxes(0, 1).reshape((b, nb * q) + a.shape[3:])


def relative_bucket(rel):
    nb = N_BUCKETS // 2
    max_exact = nb // 2
    ret = jnp.where(rel > 0, nb, 0)
    n = jnp.abs(rel)
    nf = jnp.maximum(n, 1).astype(jnp.float32)
    large = max_exact + (jnp.log(nf / max_exact) / math.log(MAX_DISTANCE / max_exact)
                         * (nb - max_exact)).astype(jnp.int32)
    large = jnp.minimum(large, nb - 1)
    return ret + jnp.where(n < max_exact, n, large)


def stick_breaking_attention(q, k, v):
    b, s, h, dh = q.shape
    nb = s // QBLOCK
    scale = dh ** -0.5
    kf = k.astype(jnp.float32)
    kpos = jnp.arange(s)

    def block(args):
        q_blk, bi = args
        qpos = bi * QBLOCK + jnp.arange(QBLOCK)
        z = jnp.einsum('bqhd,bshd->bhqs', q_blk.astype(jnp.float32), kf) * scale
        before = kpos[None, :] < qpos[:, None]
        log_keep = jnp.where(before, jax.nn.log_sigmoid(-z), 0.0)
        log_between = lax.cumsum(log_keep, axis=3, reverse=True) - log_keep
        weights = jnp.where(before, jnp.exp(jax.nn.log_sigmoid(z) + log_between), 0.0)
        return jnp.einsum('bhqs,bshd->bqhd', weights.astype(v.dtype), v)

    out = lax.map(block, (to_blocks(q, nb), jnp.arange(nb)))
    return from_blocks(out)


def dsa_sparse_attention(q, k, v, q_idx, k_idx, w_idx, rel_bias):
    b, s, h, dh = q.shape
    top = min(TOPK_MAX, s // 4)
    nb = s // QBLOCK
    k_chunk = jnp.arange(s) // CHUNK
    k_idx_f = k_idx.astype(jnp.float32)
    gather = jax.vmap(lambda arr, ix: arr[ix])

    def block(args):
        qb, qib, wb, bi = args
        qpos = bi * QBLOCK + jnp.arange(QBLOCK)
        q_chunk = qpos // CHUNK
        visible = k_chunk[None, :] <= q_chunk[:, None]
        dots = jnp.einsum('bqhi,bsi->bqhs', qib.astype(jnp.float32), k_idx_f) * IDX_DIM ** -0.5
        score = jnp.einsum('bqh,bqhs->bqs', wb.astype(jnp.float32) * IDX_HEADS ** -0.5,
                           jax.nn.relu(dots))
        score = jnp.where(visible[None], score, -jnp.inf)
        _, sel = lax.top_k(score, top)
        sel_ok = (sel // CHUNK) <= q_chunk[None, :, None]
        k_sel = gather(k, sel)
        v_sel = gather(v, sel)
        logits = jnp.einsum('bqhd,bqkhd->bhqk', qb.astype(jnp.float32),
                            k_sel.astype(jnp.float32)) * dh ** -0.5
        bias = rel_bias[relative_bucket(sel - qpos[None, :, None])]
        logits = logits + jnp.transpose(bias, (0, 3, 1, 2)).astype(jnp.float32)
        logits = jnp.where(sel_ok[:, None], logits, -jnp.inf)
        p = jax.nn.softmax(logits, axis=-1)
        return jnp.einsum('bhqk,bqkhd->bqhd', p.astype(v.dtype), v_sel)

    out = lax.map(block, (to_blocks(q, nb), to_blocks(q_idx, nb), to_blocks(w_idx, nb),
                          jnp.arange(nb)))
    return from_blocks(out)


def memory_cross_attention(x, mem, g_cross, g_mem, w_cq, w_ckv, w_co):
    b, s, _ = x.shape
    m = mem.shape[1]
    hq = rmsnorm(x, g_cross)
    hm = rmsnorm(mem, g_mem)
    q = (hq @ w_cq).reshape(b, s, MEM_HEADS, HEAD_DIM)
    km, vm = jnp.split(hm @ w_ckv, 2, axis=-1)
    km = km.reshape(b, m, MEM_HEADS, HEAD_DIM)
    vm = vm.reshape(b, m, MEM_HEADS, HEAD_DIM)
    logits = jnp.einsum('bshd,bmhd->bhsm', q.astype(jnp.float32),
                        km.astype(jnp.float32)) * HEAD_DIM ** -0.5
    p = jax.nn.softmax(logits, axis=-1)
    o = jnp.einsum('bhsm,bmhd->bshd', p.astype(vm.dtype), vm).reshape(b, s, MEM_WIDTH)
    return o @ w_co


def conv_ffn(x, g_ffn, w_up, conv_w, conv_b, w_down):
    s = x.shape[1]
    h = rmsnorm(x, g_ffn)
    u = h @ w_up
    up = jnp.pad(u, ((0, 0), (CONV_WIDTH - 1, 0), (0, 0)))
    c = conv_b
    for i in range(CONV_WIDTH):
        c = c + conv_w[i] * up[:, i:i + s]
    a, val = jnp.split(c, 2, axis=-1)
    return (jax.nn.gelu(a) * val) @ w_down


def hybrid_layer(x, mem, g_mix, w_in, b_gate, w_proj_sb, w_proj_dsa, w_out, rel_bias,
                 g_cross, g_mem, w_cq, w_ckv, w_co, g_ffn, w_up, conv_w, conv_b, w_down):
    b, s, _ = x.shape
    h = rmsnorm(x, g_mix)
    u = h @ w_in
    offsets = []
    acc = 0
    for wdt in IN_WIDTHS[:-1]:
        acc += wdt
        offsets.append(acc)
    (q_sb, k_sb, v_sb, q_ds, k_ds, v_ds, q_ix, k_ix, w_ix, gates) = jnp.split(u, offsets, axis=-1)
    hd = lambda a, n, d: a.reshape(b, s, n, d)
    o_sb = stick_breaking_attention(hd(q_sb, N_HEADS_SB, HEAD_DIM), hd(k_sb, N_HEADS_SB, HEAD_DIM),
                                    hd(v_sb, N_HEADS_SB, HEAD_DIM))
    o_ds = dsa_sparse_attention(hd(q_ds, N_HEADS_DSA, HEAD_DIM), hd(k_ds, N_HEADS_DSA, HEAD_DIM),
                                hd(v_ds, N_HEADS_DSA, HEAD_DIM), hd(q_ix, IDX_HEADS, IDX_DIM),
                                k_ix, w_ix, rel_bias)
    o_sb = o_sb.reshape(b, s, SB_WIDTH) @ w_proj_sb
    o_ds = o_ds.reshape(b, s, DSA_WIDTH) @ w_proj_dsa
    g = jax.nn.sigmoid((gates + b_gate).astype(jnp.float32)).astype(x.dtype)
    g_sb, g_ds = jnp.split(g, 2, axis=-1)
    x = x + (g_sb * o_sb + g_ds * o_ds) @ w_out
    x = x + memory_cross_attention(x, mem, g_cross, g_mem, w_cq, w_ckv, w_co)
    x = x + conv_ffn(x, g_ffn, w_up, conv_w, conv_b, w_down)
    return x


def setup_inputs(seed: int = 0) -> dict:
    key = jax.random.key(seed)
    ks = jax.random.split(key, 24)
    f32 = jnp.float32
    nrm = lambda k, shape, fan_in: jax.random.normal(k, shape, f32) * fan_in ** -0.5
    gain = lambda k: 1.0 + 0.01 * jax.random.normal(k, (DEPTH, D_MODEL), f32)
    return {
        "x": jax.random.normal(ks[0], (BATCH, SEQ, D_MODEL), f32),
        "mem": jax.random.normal(ks[1], (BATCH, N_MEM, D_MODEL), f32),
        "g_mix": gain(ks[2]),
        "w_in": nrm(ks[3], (DEPTH, D_MODEL, IN_TOTAL), D_MODEL),
        "b_gate": 0.01 * jax.random.normal(ks[4], (DEPTH, 2 * D_MODEL), f32),
        "w_proj_sb": nrm(ks[5], (DEPTH, SB_WIDTH, D_MODEL), SB_WIDTH),
        "w_proj_dsa": nrm(ks[6], (DEPTH, DSA_WIDTH, D_MODEL), DSA_WIDTH),
        "w_out": nrm(ks[7], (DEPTH, D_MODEL, D_MODEL), D_MODEL),
        "rel_bias": 0.2 * jax.random.normal(ks[8], (N_BUCKETS, N_HEADS_DSA), f32),
        "g_cross": gain(ks[9]),
        "g_mem": gain(ks[10]),
        "w_cq": nrm(ks[11], (DEPTH, D_MODEL, MEM_WIDTH), D_MODEL),
        "w_ckv": nrm(ks[12], (DEPTH, D_MODEL, 2 * MEM_WIDTH), D_MODEL),
        "w_co": nrm(ks[13], (DEPTH, MEM_WIDTH, D_MODEL), MEM_WIDTH),
        "g_ffn": gain(ks[14]),
        "w_up": nrm(ks[15], (DEPTH, D_MODEL, 2 * D_FF), D_MODEL),
        "conv_w": nrm(ks[16], (DEPTH, CONV_WIDTH, 2 * D_FF), CONV_WIDTH),
        "conv_b": 0.01 * jax.random.normal(ks[17], (DEPTH, 2 * D_FF), f32),
        "w_down": nrm(ks[18], (DEPTH, D_FF, D_MODEL), D_FF),
        "g_final": 1.0 + 0.01 * jax.random.normal(ks[19], (D_MODEL,), f32),
    }


def reference(x, mem, g_mix, w_in, b_gate, w_proj_sb, w_proj_dsa, w_out, rel_bias,
              g_cross, g_mem, w_cq, w_ckv, w_co, g_ffn, w_up, conv_w, conv_b, w_down, g_final):
    for l in range(DEPTH):
        x = hybrid_layer(x, mem, g_mix[l], w_in[l], b_gate[l], w_proj_sb[l], w_proj_dsa[l],
                         w_out[l], rel_bias, g_cross[l], g_mem[l], w_cq[l], w_ckv[l], w_co[l],
                         g_ffn[l], w_up[l], conv_w[l], conv_b[l], w_down[l])
    return rmsnorm(x, g_final)
```

```python
import numpy as np
from contextlib import ExitStack, contextmanager
import concourse.bass as bass
import concourse.mybir as mybir
from concourse.bass_utils import run_bass_kernel_spmd

F32 = mybir.dt.float32
BF16 = mybir.dt.bfloat16
AF = mybir.ActivationFunctionType
ALU = mybir.AluOpType
AX = mybir.AxisListType

D = 2048
KC = 16
NQ = 1026
OWN0 = 3070
QS = [(0, 342), (342, 342), (684, 342)]
EPS = 1e-6
NIT = 24
TW = 1000
J0 = 404
DFF = 6144
DEBUG = False
WARM_N = 16


class Res:
    __slots__ = ("w", "r")

    def __init__(self):
        self.w = None
        self.r = []


class Eng:
    def __init__(self, name, obj, sem, is_pe=False):
        self.name = name
        self.obj = obj
        self.sem = sem
        self.count = 0
        self.waited = {}
        self.is_pe = is_pe


class FW:
    def __init__(self, nc, ctx, n_dsem=12):
        self.nc = nc
        mk = lambda n: ctx.enter_context(nc.semaphore(n))
        self.pe = Eng("pe", nc.tensor, mk("s_pe"), is_pe=True)
        self.act = Eng("act", nc.scalar, mk("s_act"))
        self.dve = Eng("dve", nc.vector, mk("s_dve"))
        self.pool = Eng("pool", nc.gpsimd, mk("s_pool"))
        self.sp = Eng("sp", nc.sync, mk("s_sp"))
        self.engs = [self.pe, self.act, self.dve, self.pool, self.sp]
        self.dsems = {}
        for q in ("sp", "pool", "act"):
            self.dsems[q] = [[mk(f"d_{q}{i}"), 0] for i in range(n_dsem if q != "act" else 4)]
        self.dma_i = {"sp": 0, "pool": 0, "act": 0}
        self.n_inst = 0

    def _wait(self, eng, tok):
        if tok is None:
            return
        sem, val, owner = tok
        if owner is eng and eng.is_pe:
            return
        key = id(sem)
        if eng.waited.get(key, 0) >= val:
            return
        eng.obj.wait_ge(sem, val)
        eng.waited[key] = val

    def _deps(self, eng, reads, writes):
        for r in reads:
            self._wait(eng, r.w)
        for w in writes:
            self._wait(eng, w.w)
            for t in w.r:
                self._wait(eng, t)

    def _record(self, tok, reads, writes):
        for r in reads:
            r.r.append(tok)
            if len(r.r) > 24:
                best = {}
                for t in r.r:
                    k = id(t[0])
                    if k not in best or best[k][1] < t[1]:
                        best[k] = t
                r.r = list(best.values())
        for w in writes:
            w.w = tok
            w.r = []

    def op(self, eng, fn, reads=(), writes=()):
        self._deps(eng, reads, writes)
        inst = fn(eng.obj)
        eng.count += 1
        inst.then_inc(eng.sem, 1)
        tok = (eng.sem, eng.count, eng)
        self._record(tok, reads, writes)
        self.n_inst += 1
        return tok

    def dma(self, out, in_, reads=(), writes=(), q="sp"):
        eng = {"sp": self.sp, "pool": self.pool, "act": self.act}[q]
        slots = self.dsems[q]
        i = self.dma_i[q]
        self.dma_i[q] = i + 1
        slot = slots[i % len(slots)]
        if slot[1] > 0:
            self._wait(eng, (slot[0], slot[1], None))
        self._deps(eng, reads, writes)
        inst = eng.obj.dma_start(out=out, in_=in_)
        slot[1] += 16
        inst.then_inc(slot[0], 16)
        tok = (slot[0], slot[1], None)
        self._record(tok, reads, writes)
        self.n_inst += 1
        return tok

    def all_tokens(self):
        toks = []
        for e in self.engs:
            if e.count > 0:
                toks.append((e.sem, e.count, e))
        for q in self.dsems:
            for s in self.dsems[q]:
                if s[1] > 0:
                    toks.append((s[0], s[1], None))
        return toks

    def barrier(self):
        toks = self.all_tokens()
        for e in self.engs:
            for t in toks:
                if t[2] is e:
                    continue
                self._wait(e, t)

    def finish(self):
        for t in self.all_tokens():
            self._wait(self.sp, t)


def build():
    nc = bass.Bass("TRN2", target_bir_lowering=False)

    def din(name, shape, dtype=F32):
        return nc.dram_tensor(name, shape, dtype, kind="ExternalInput").ap()

    xw = din("xw", [128, KC, 4096])
    memT = din("memT", [128, KC, 256])
    w_in = din("w_in", [D, 11344]).rearrange("(kc p) n -> p kc n", p=128)
    w_psb = din("w_proj_sb", [1024, D]).rearrange("(kc p) n -> p kc n", p=128)
    w_pds = din("w_proj_dsa", [1024, D]).rearrange("(kc p) n -> p kc n", p=128)
    w_out = din("w_out", [D, D]).rearrange("(kc p) n -> p kc n", p=128)
    w_cq = din("w_cq", [D, 512]).rearrange("(kc p) n -> p kc n", p=128)
    w_ckv = din("w_ckv", [D, 1024]).rearrange("(kc p) n -> p kc n", p=128)
    w_co = din("w_co", [512, D]).rearrange("(kc p) n -> p kc n", p=128)
    w_up = din("w_up", [D, 2 * DFF]).rearrange("(kc p) n -> p kc n", p=128)
    w_down = din("w_down", [DFF, D]).rearrange("(kc p) n -> p kc n", p=128)
    gvec = din("gvec", [128, 5, KC])
    bgate = din("bgate", [128, 32])
    convw = din("convw", [128, 3, 96])
    convb = din("convb", [128, 96])
    dmask = din("dmask", [NQ, 4096])
    tbias = din("tbias", [128, 8, TW])
    rb15 = din("rb15", [128, 8])
    sbm_d = din("sbm", [128, 11, 342])
    cmat = din("cmat", [128, 4, 128])
    hv_d = din("hv", [128, 1])
    ckv_d = din("ckv", [128, NIT])
    outT = nc.dram_tensor("outT", [128, KC, 1024], F32, kind="ExternalOutput").ap()

    skind = "ExternalOutput" if DEBUG else "Internal"
    KT = [nc.dram_tensor(f"KT{i}", [8, 128, 4096], BF16, kind=skind).ap() for i in range(2)]
    VS = [nc.dram_tensor(f"VS{i}", [4096, 1024], BF16, kind=skind).ap() for i in range(2)]
    QT = [nc.dram_tensor(f"QT{i}", [8, 128, NQ], BF16, kind=skind).ap() for i in range(3)]
    X2S = nc.dram_tensor("X2S", [128, KC, NQ], F32, kind=skind).ap()
    SELS = [nc.dram_tensor(f"SELS{i}", [128, 32, 342], BF16, kind=skind).ap() for i in range(3)]
    X3S = nc.dram_tensor("X3S", [128, KC, 1024], F32, kind=skind).ap()
    OSC = [nc.dram_tensor(f"OSC{i}", [128, 8, NQ], BF16, kind=skind).ap() for i in range(2)]
    if DEBUG:
        DBG = nc.dram_tensor("DBG", [128, 16, NQ], F32, kind="ExternalOutput").ap()

    with ExitStack() as ctx:
        fw = FW(nc, ctx)
        pe, act, dve, pool = fw.pe, fw.act, fw.dve, fw.pool

        ARENA = 52800
        big = ctx.enter_context(nc.sbuf_tensor("arena", [128, ARENA], F32))

        class Stk:
            def __init__(self, left):
                self.left = left
                self.top = 0 if left else ARENA * 4

            def alloc(self, shape, dtype):
                nel = 1
                for d_ in shape[1:]:
                    nel *= d_
                nb = nel * (2 if dtype == BF16 else 4)
                nb = (nb + 63) // 64 * 64
                if self.left:
                    off = self.top
                    self.top += nb
                else:
                    self.top -= nb
                    off = self.top
                assert LS.top <= RS.top, ("SBUF arena overflow", LS.top, RS.top)
                gapmin[0] = min(gapmin[0], RS.top - LS.top)
                v = big[:, off // 4: (off + nb) // 4]
                if dtype == BF16:
                    v = v.bitcast(BF16)
                v = v[:, 0:nel]
                if len(shape) == 3:
                    v = v.rearrange("p (a b) -> p a b", b=shape[2])
                return v

        gapmin = [1 << 30]
        LS = Stk(True)
        RS = Stk(False)

        @contextmanager
        def scope(stk):
            m_ = stk.top
            yield stk
            stk.top = m_

        def sbt(c, name, shape, dtype):
            return c.alloc(shape, dtype), Res()

        def pst(name, shape, dtype):
            return ctx.enter_context(nc.psum_tensor(name, shape, dtype)), Res()

        pgen = [pst(f"pg{i}", [128, 512], F32) for i in range(3)]
        pgen4 = pst("pg4", [128, 512], F32)
        pgen5 = pst("pg5", [128, 512], F32)
        pO = [pst(f"po{i}", [128, 512], F32) for i in range(2)]
        pS, rpS = pgen4
        pT, rpT = pst("ptr", [128, 1024], BF16)
        pgi = [0]

        def next_ps():
            p = pgen[pgi[0] % 3]
            pgi[0] += 1
            return p

        cm, rcm = sbt(LS, "cm", [128, 4, 128], BF16)
        gv, rgv = sbt(LS, "gv", [128, 5, KC], F32)
        cst, rcst = sbt(LS, "cst", [128, 4], F32)
        hvt, rhv = sbt(LS, "hvt", [128, 1], F32)
        L_consts = LS.top
        Ltri, nones, ones, ident = cm[:, 0, :], cm[:, 1, :], cm[:, 2, :], cm[:, 3, :]
        fw.dma(cm[:], cmat, writes=[rcm], q="pool")
        fw.dma(gv[:], gvec, writes=[rgv])
        fw.dma(hvt[:], hv_d, writes=[rhv])
        fw.op(dve, lambda e: e.memset(cst[:, 0:1], EPS), writes=[rcst])
        fw.op(dve, lambda e: e.memset(cst[:, 1:2], 1.0), writes=[rcst])
        fw.op(dve, lambda e: e.memset(cst[:, 2:3], -30000.0), writes=[rcst])
        fw.op(dve, lambda e: e.memset(cst[:, 3:4], 1e-30), writes=[rcst])

        def mm(out, lhsT, rhs, start, stop, reads, writes):
            return fw.op(pe, lambda e: e.matmul(out, lhsT, rhs, start=start, stop=stop), reads, writes)

        def actf(out, in_, func, reads, writes, **kw):
            return fw.op(act, lambda e: e.activation(out, in_, func, **kw), reads, writes)

        evi = [0]

        def evac(dst, src, reads, writes, scale=None):
            evi[0] += 1
            if evi[0] % 2 == 0:
                if scale is None:
                    return actf(dst, src, AF.Copy, reads, writes)
                return actf(dst, src, AF.Copy, reads, writes, scale=float(scale))
            if scale is None:
                return fw.op(dve, lambda e: e.tensor_copy(dst, src), reads, writes)
            return fw.op(dve, lambda e: e.tensor_scalar(dst, src, float(scale), None, op0=ALU.mult), reads, writes)

        def load_w(dst, rdst, src, nkc):
            h = max(1, nkc // 2)
            for a in range(0, nkc, h):
                fw.dma(dst[:, a:a + h, :], src[:, a:a + h, :], writes=[rdst], q="pool")

        def rms_core(c_sq, xs, rxs, ntok, gi, dst_fn, dst_res):
            sq, rsq, lnv, rln, rs, rrs = c_sq
            actf(sq[:, :, 0:ntok], xs[:, :, 0:ntok], AF.Square, [rxs], [rsq])
            for kc in range(KC):
                mm(pS[:, 0:ntok], ones, sq[:, kc, 0:ntok], kc == 0, kc == KC - 1, [rsq, rcm], [rpS])
            actf(lnv[:, 0:ntok], pS[:, 0:ntok], AF.Ln, [rpS, rcst], [rln], bias=cst[:, 0:1], scale=1.0 / D)
            actf(rs[:, 0:ntok], lnv[:, 0:ntok], AF.Exp, [rln], [rrs], scale=-0.5)
            for kc in range(KC):
                fw.op(dve, lambda e: e.scalar_tensor_tensor(dst_fn(kc), xs[:, kc, 0:ntok], gv[:, gi, kc:kc + 1],
                                                            rs[:, 0:ntok], op0=ALU.mult, op1=ALU.mult),
                      [rxs, rrs, rgv], dst_res)

        def mk_sq(c, w):
            sq, rsq = sbt(c, "sq", [128, KC, w], BF16)
            lnv, rln = sbt(c, "lnv", [128, w], F32)
            rs, rrs = sbt(c, "rs", [128, w], F32)
            return (sq, rsq, lnv, rln, rs, rrs)

        kix, rkix = sbt(LS, "kix", [128, 4096], BF16)
        wabs, rwabs = sbt(LS, "wabs", [128, 9, 16], F32)
        wsgn, rwsgn = sbt(LS, "wsgn", [128, 9, 16], F32)
        QT_tiles = []
        for s, (i0, n) in enumerate(QS):
            for m, (a, nq) in enumerate([(0, 128), (128, 128), (256, 86)]):
                QT_tiles.append((s, i0 + a, nq))

        with scope(RS) as c1:
            hT, _ = sbt(c1, "hT", [128, KC, 2048], BF16)
            hres = [Res() for _ in range(8)]
            xsb = [sbt(c1, f"xs{i}", [128, KC, 256], F32) for i in range(2)]
            csq = mk_sq(c1, 256)
            wts = [sbt(c1, f"wt{i}", [128, KC, 256], BF16) for i in range(4)]
            stg = [sbt(c1, f"stg{i}", [128, 512], BF16) for i in range(4)]
            wi = [0]
            si = [0]

            def nwt():
                w = wts[wi[0] % 4]
                wi[0] += 1
                return w

            def nstg():
                s_ = stg[si[0] % 4]
                si[0] += 1
                return s_

            def hr(a, b):
                return hres[a // 256:(b - 1) // 256 + 1]

            for g in range(2):
                for sl in range(8):
                    xs, rxs = xsb[sl % 2]
                    fw.dma(xs[:], xw[:, :, g * 2048 + sl * 256: g * 2048 + sl * 256 + 256], writes=[rxs])
                    rms_core(csq, xs, rxs, 256, 0, lambda kc: hT[:, kc, sl * 256: sl * 256 + 256], [hres[sl]])
                for br, (koff, voff) in enumerate([(1024, 2048), (4096, 5120)]):
                    for h in range(8):
                        wt, rwt = nwt()
                        load_w(wt[:, :, 0:128], rwt, w_in[:, :, koff + 128 * h: koff + 128 * h + 128], KC)
                        for s4 in range(4):
                            p, rp = next_ps()
                            for kc in range(KC):
                                mm(p[:, 0:512], wt[:, kc, 0:128], hT[:, kc, 512 * s4: 512 * s4 + 512], kc == 0, kc == KC - 1,
                                   [rwt] + hr(512 * s4, 512 * s4 + 512), [rp])
                            st, rst = nstg()
                            evac(st[:, 0:512], p[:, 0:512], [rp], [rst])
                            fw.dma(KT[br][h, :, g * 2048 + 512 * s4: g * 2048 + 512 * s4 + 512], st[:, 0:512], reads=[rst])
                    for cg in range(4):
                        wt, rwt = nwt()
                        load_w(wt[:, :, 0:256], rwt, w_in[:, :, voff + 256 * cg: voff + 256 * cg + 256], KC)
                        for blk in range(16):
                            p, rp = next_ps()
                            for kc in range(KC):
                                mm(p[:, 0:256], hT[:, kc, 128 * blk: 128 * blk + 128], wt[:, kc, 0:256], kc == 0, kc == KC - 1,
                                   [rwt] + hr(128 * blk, 128 * blk + 128), [rp])
                            st, rst = nstg()
                            evac(st[:, 0:256], p[:, 0:256], [rp], [rst])
                            r0 = g * 2048 + 128 * blk
                            fw.dma(VS[br][r0:r0 + 128, 256 * cg: 256 * cg + 256], st[:, 0:256], reads=[rst])
                wt, rwt = nwt()
                fw.dma(wt[:, :, 0:64], w_in[:, :, 7168:7232], writes=[rwt], q="pool")
                fw.dma(wt[:, :, 64:128], w_in[:, :, 7168:7232], writes=[rwt], q="pool")
                for s4 in range(4):
                    p, rp = next_ps()
                    for kc in range(KC):
                        mm(p[:, 0:512], wt[:, kc, 0:128], hT[:, kc, 512 * s4: 512 * s4 + 512], kc == 0, kc == KC - 1,
                           [rwt] + hr(512 * s4, 512 * s4 + 512), [rp])
                    evac(kix[:, g * 2048 + 512 * s4: g * 2048 + 512 * s4 + 512], p[:, 0:512], [rp], [rkix])
                if g == 1:
                    OB = 1022
                    for qi, (off, scale) in enumerate([(0, 128 ** -0.5), (3072, 128 ** -0.5), (6144, 0.125)]):
                        for h in range(8):
                            wt, rwt = nwt()
                            load_w(wt[:, :, 0:128], rwt, w_in[:, :, off + 128 * h: off + 128 * h + 128], KC)
                            for (i0, n) in QS:
                                p, rp = next_ps()
                                for kc in range(KC):
                                    mm(p[:, 0:n], wt[:, kc, 0:128], hT[:, kc, OB + i0: OB + i0 + n], kc == 0, kc == KC - 1,
                                       [rwt] + hr(OB + i0, OB + i0 + n), [rp])
                                st, rst = nstg()
                                evac(st[:, 0:n], p[:, 0:n], [rp], [rst], scale=scale)
                                fw.dma(QT[qi][h, :, i0:i0 + n], st[:, 0:n], reads=[rst])
                    wt, rwt = nwt()
                    fw.dma(wt[:, :, 0:16], w_in[:, :, 7232:7248], writes=[rwt], q="pool")
                    for qt, (s, iq, nq) in enumerate(QT_tiles):
                        p, rp = next_ps()
                        for kc in range(KC):
                            mm(p[0:nq, 0:16], hT[:, kc, OB + iq: OB + iq + nq], wt[:, kc, 0:16], kc == 0, kc == KC - 1,
                               [rwt] + hr(OB + iq, OB + iq + nq), [rp])
                        actf(wabs[0:nq, qt, :], p[0:nq, 0:16], AF.Abs, [rp], [rwabs], scale=0.25)
                        actf(wsgn[0:nq, qt, :], p[0:nq, 0:16], AF.Sign, [rp], [rwsgn])
        fw.barrier()

        L_kix = LS.top
        o_sb, ro_sb = sbt(LS, "o_sb", [128, 8, NQ], BF16)

        rSEL = [Res() for _ in range(3)]
        TILE_GEOM = [(0, 128), (128, 128), (256, 86)]

        def geom(s):
            i0, n = QS[s]
            tau_min = OWN0 + i0
            tau_max = OWN0 + i0 + n - 1
            vis_end = (tau_max // 64 + 1) * 64
            nblk = min((vis_end + 127) // 128, 32)
            nks = min((vis_end + 511) // 512, 8)
            return i0, n, tau_min, nblk, nks

        with scope(RS) as c2:
            sbm, rsbm = sbt(c2, "sbm", [128, 11, 342], BF16)
            fw.dma(sbm[:], sbm_d, writes=[rsbm], q="pool")
            kb = [sbt(c2, f"kb{i}", [128, 4096], BF16) for i in range(2)]
            vb = [sbt(c2, f"vb{i}", [128, 32, 128], BF16) for i in range(2)]
            qb = [sbt(c2, f"qb{i}", [128, NQ], BF16) for i in range(2)]
            eb = [sbt(c2, f"e{i}", [128, 342], F32) for i in range(3)]
            spb = [sbt(c2, f"sp{i}", [128, 342], BF16) for i in range(4)]
            spmb = [sbt(c2, f"spm{i}", [128, 342], BF16) for i in range(4)]
            wb_ = [sbt(c2, f"w{i}", [128, 342], BF16) for i in range(3)]
            wmb = [sbt(c2, f"wm{i}", [128, 342], BF16) for i in range(3)]
            accb = [sbt(c2, f"acc{i}", [128, 342], BF16) for i in range(3)]
            qixb = [sbt(c2, f"qix{i}", [128, 8, 342], BF16) for i in range(2)]
            ckt, rckt = sbt(c2, "ckt", [128, NIT], F32)
            fw.dma(ckt, ckv_d, writes=[rckt])
            accs2 = [sbt(c2, f"dacc{i}", [128, 4096], F32) for i in range(2)]
            dm16b = [sbt(c2, f"dm16{i}", [128, 4096], BF16) for i in range(2)]
            selqb = [sbt(c2, f"selq{i}", [128, 4096], BF16) for i in range(2)]
            selT, rselT = sbt(c2, "selTs", [128, 32, 342], BF16)
            rb_ = [sbt(c2, f"dr{i}", [128, 512], BF16) for i in range(4)]
            dgb = [sbt(c2, f"dg{i}", [128, 16, 128], BF16) for i in range(2)]
            sm, rsm = sbt(c2, "dsm", [128, 8], F32)
            wall, rwall = sbt(c2, "wall", [128, NIT], F32)

            def sel_gen():
                tiles = [(s_, m_) for s_ in range(3) for m_ in range(3)]

                def build_dg(qt2, hhs=range(16)):
                    nq2 = TILE_GEOM[qt2 % 3][1]
                    dg2, rdg2 = dgb[qt2 % 2]
                    for hh in hhs:
                        fw.op(pool, lambda e: e.tensor_scalar(dg2[0:nq2, hh, 0:nq2], ident[0:nq2, 0:nq2], wsgn[0:nq2, qt2, hh:hh + 1], None, op0=ALU.mult),
                              [rcm, rwsgn], [rdg2])

                def load_qix(s2):
                    i02, n2 = QS[s2]
                    qx, rqx = qixb[s2 % 2]
                    for hp in range(8):
                        fw.dma(qx[:, hp, 0:n2], QT[2][hp, :, i02:i02 + n2], writes=[rqx])

                def part1(s, m):
                    i0, n, tau_min, nblk, nks = geom(s)
                    W = 512 * nks
                    a, nq = TILE_GEOM[m]
                    qt = s * 3 + m
                    iq = i0 + a
                    qix, rqix = qixb[s % 2]
                    if m == 0 and s + 1 < 3:
                        load_qix(s + 1)
                    acc, racc = accs2[qt % 2]
                    selq, rselq = selqb[qt % 2]
                    dm, rdm = dm16b[qt % 2]
                    fw.dma(dm[0:nq, 0:W], dmask[iq:iq + nq, 0:W], writes=[rdm], q="pool")
                    dg, rdg = dgb[qt % 2]
                    seq = [(ks, hh) for ks in range(nks) for hh in range(16)]
                    rr = {}

                    def iA(i):
                        ks, hh = seq[i]
                        hp, half = hh // 2, hh % 2
                        p, rp = next_ps()
                        mm(p[0:nq, 0:512], qix[64 * half:64 * half + 64, hp, a:a + nq],
                           kix[64 * half:64 * half + 64, 512 * ks:512 * ks + 512], True, True, [rqix, rkix], [rp])
                        r_, rr_ = rb_[i % 4]
                        actf(r_[0:nq, :], p[0:nq, 0:512], AF.Relu, [rp, rwabs], [rr_], scale=wabs[0:nq, qt, hh:hh + 1])
                        rr[i] = (r_, rr_)

                    def iB(i):
                        ks, hh = seq[i]
                        r_, rr_ = rr.pop(i)
                        psc, rpsc = pO[ks % 2]
                        mm(psc[0:nq, 0:512], dg[0:nq, hh, 0:nq], r_[0:nq, :], hh == 0, False, [rdg, rr_], [rpsc])
                        if hh == 15:
                            mm(psc[0:nq, 0:512], ident[0:nq, 0:nq], dm[0:nq, 512 * ks:512 * ks + 512], False, True, [rcm, rdm], [rpsc])
                            actf(acc[0:nq, 512 * ks:512 * ks + 512], psc[0:nq, 0:512], AF.Copy, [rpsc], [racc])

                    ns = len(seq)
                    iA(0)
                    iA(1)
                    for i in range(ns):
                        if i + 2 < ns:
                            iA(i + 2)
                        iB(i)
                        if qt + 1 < 9 and i % 6 == 0 and i // 6 < 16:
                            build_dg(qt + 1, [i // 6])
                        yield
                    fw.op(dve, lambda e: e.tensor_reduce(sm[0:nq, 0:1], acc[0:nq, 0:W], axis=AX.X, op=ALU.max), [racc], [rsm])
                    fw.op(dve, lambda e: e.scalar_tensor_tensor(selq[0:nq, 0:W], acc[0:nq, 0:W], -1e29, acc[0:nq, 0:W],
                                                                op0=ALU.is_gt, op1=ALU.mult), [racc], [rselq])
                    fw.op(dve, lambda e: e.tensor_reduce(sm[0:nq, 1:2], selq[0:nq, 0:W], axis=AX.X, op=ALU.min), [rselq], [rsm])
                    fw.op(dve, lambda e: e.tensor_scalar(sm[0:nq, 2:3], sm[0:nq, 1:2], 0.0, 1.02, op0=ALU.min, op1=ALU.mult), [rsm], [rsm])
                    fw.op(dve, lambda e: e.tensor_scalar(sm[0:nq, 2:3], sm[0:nq, 2:3], -1.0, None, op0=ALU.add), [rsm], [rsm])
                    fw.op(dve, lambda e: e.scalar_tensor_tensor(sm[0:nq, 3:4], sm[0:nq, 0:1], 1.0, sm[0:nq, 2:3],
                                                                op0=ALU.add, op1=ALU.subtract), [rsm], [rsm])
                    fw.op(dve, lambda e: e.tensor_scalar(wall[0:nq, :], ckt[0:nq, :], sm[0:nq, 3:4], None, op0=ALU.mult), [rsm, rckt], [rwall])
                    fw.op(dve, lambda e: e.tensor_tensor(sm[0:nq, 4:5], sm[0:nq, 2:3], wall[0:nq, 0:1], op=ALU.add), [rsm, rwall], [rsm])
                    for it in range(NIT):
                        fw.op(dve, lambda e: e.tensor_scalar(selq[0:nq, 0:W], acc[0:nq, 0:W], sm[0:nq, 4:5], 0.0,
                                                             op0=ALU.is_gt, op1=ALU.add, accum_out=sm[0:nq, 5:6]),
                              [racc, rsm], [rselq, rsm])
                        off = -0.5 if it < NIT - 1 else -1.0
                        fw.op(dve, lambda e: e.tensor_scalar(sm[0:nq, 6:7], sm[0:nq, 5:6], 256.0, off,
                                                             op0=ALU.is_ge, op1=ALU.add), [rsm], [rsm])
                        fw.op(dve, lambda e: e.scalar_tensor_tensor(sm[0:nq, 4:5], sm[0:nq, 6:7], wall[0:nq, it:it + 1], sm[0:nq, 4:5],
                                                                    op0=ALU.mult, op1=ALU.add), [rsm, rwall], [rsm])
                    fw.op(dve, lambda e: e.tensor_scalar(selq[0:nq, 0:W], acc[0:nq, 0:W], sm[0:nq, 4:5], None, op0=ALU.is_gt),
                          [racc, rsm], [rselq])

                def part2(s, m):
                    i0, n, tau_min, nblk, nks = geom(s)
                    a, nq = TILE_GEOM[m]
                    selq, rselq = selqb[(s * 3 + m) % 2]
                    for b0 in range(0, nblk, 4):
                        nb_ = min(4, nblk - b0)
                        for j in range(nb_):
                            b = b0 + j
                            fw.op(pe, lambda e: e.transpose(pT[:, 128 * j:128 * j + nq], selq[0:nq, 128 * b:128 * b + 128], ident[0:nq, 0:nq]),
                                  [rselq, rcm], [rpT])
                        src = pT[:, 0:128 * nb_].rearrange("p (j q) -> p j q", q=128)[:, :, 0:nq]
                        actf(selT[:, b0:b0 + nb_, a:a + nq], src, AF.Identity, [rpT, rcst], [rselT], scale=30000.0, bias=cst[:, 2:3])
                        yield
                    if m == 2:
                        fw.dma(SELS[s][:, 0:nblk, 0:n], selT[:, 0:nblk, 0:n], reads=[rselT], writes=[rSEL[s]], q="act")

                load_qix(0)
                build_dg(0)
                yield from part1(*tiles[0])
                for i, (s_, m_) in enumerate(tiles):
                    if i + 1 < len(tiles):
                        yield from part1(*tiles[i + 1])
                    yield from part2(s_, m_)

            selg = sel_gen()
            sel_steps = 0
            for s_ in range(3):
                _, _, _, nblk_, nks_ = geom(s_)
                sel_steps += 3 * (nks_ * 16 + (nblk_ + 3) // 4)
            sb_blocks = 8 * sum((OWN0 + i0 + n - 2) // 128 + 1 for (i0, n) in QS)
            pump_state = [0, 0]

            def pump():
                pump_state[1] += 1
                target = min(sel_steps, (sel_steps * pump_state[1] * 100) // (sb_blocks * 76) + 2)
                while pump_state[0] < target:
                    try:
                        next(selg)
                    except StopIteration:
                        pump_state[0] = 1 << 30
                        return
                    pump_state[0] += 1

            mask_idx = {}
            mi = 0
            for s, (i0, n) in enumerate(QS):
                b_hi = (OWN0 + i0 + n - 2) // 128
                b_full = (OWN0 + i0 - 128) // 128
                for b in range(b_full + 1, b_hi + 1):
                    mask_idx[(s, b)] = mi
                    mi += 1
            assert mi == 11, mi
            for h in range(8):
                kt, rkt = kb[h % 2]
                vt, rvt = vb[h % 2]
                qt_, rqt = qb[h % 2]
                fw.dma(kt[:], KT[0][h], writes=[rkt])
                fw.dma(vt[:], VS[0].rearrange("(b p) f -> p b f", p=128)[:, :, 128 * h:128 * h + 128], writes=[rvt])
                fw.dma(qt_[:], QT[0][h], writes=[rqt])
                for s, (i0, n) in enumerate(QS):
                    b_hi = (OWN0 + i0 + n - 2) // 128
                    blocks = list(range(b_hi, -1, -1))
                    po, rpo = (pgen4, pgen5)[(h * 3 + s) % 2]
                    q_ap = qt_[:, i0:i0 + n]
                    st = {}

                    def stage1(b, k):
                        pz, rpz = next_ps()
                        mm(pz[:, 0:n], kt[:, 128 * b:128 * b + 128], q_ap, True, True, [rkt, rqt], [rpz])
                        e_, re_ = eb[k % 3]
                        actf(e_[:, 0:n], pz[:, 0:n], AF.Exp, [rpz], [re_])
                        sp_, rsp = spb[k % 4]
                        actf(sp_[:, 0:n], e_[:, 0:n], AF.Ln, [re_, rcst], [rsp], bias=cst[:, 1:2], scale=1.0)
                        if (s, b) in mask_idx:
                            m_ = sbm[:, mask_idx[(s, b)], 0:n]
                            spm, rspm = spmb[k % 4]
                            fw.op(pool, lambda e: e.tensor_tensor(spm[:, 0:n], sp_[:, 0:n], m_, op=ALU.mult), [rsp, rsbm], [rspm])
                            st[b] = (spm, rspm)
                        else:
                            st[b] = (sp_, rsp)

                    accs = {}
                    wms = {}

                    def stage2a(b, k):
                        spm, rspm = st[b]
                        first = (k == 0)
                        if k == 1:
                            accs[k] = st[blocks[0]]
                        elif k >= 2:
                            an, ran = accb[k % 3]
                            ap_, rap = accs[k - 1]
                            sprev, rsprev = st[blocks[k - 1]]
                            fw.op(pool, lambda e: e.tensor_tensor(an[:, 0:n], ap_[:, 0:n], sprev[:, 0:n], op=ALU.add), [rap, rsprev], [ran])
                            accs[k] = (an, ran)
                        pa, rpa = next_ps()
                        mm(pa[:, 0:n], kt[:, 128 * b:128 * b + 128], q_ap, True, False, [rkt, rqt], [rpa])
                        mm(pa[:, 0:n], Ltri, spm[:, 0:n], False, first, [rcm, rspm], [rpa])
                        if not first:
                            ap_, rap = accs[k]
                            mm(pa[:, 0:n], nones, ap_[:, 0:n], False, True, [rcm, rap], [rpa])
                        w_, rw_ = wb_[k % 3]
                        actf(w_[:, 0:n], pa[:, 0:n], AF.Exp, [rpa], [rw_])
                        if (s, b) in mask_idx:
                            m_ = sbm[:, mask_idx[(s, b)], 0:n]
                            wm, rwm = wmb[k % 3]
                            fw.op(pool, lambda e: e.tensor_tensor(wm[:, 0:n], w_[:, 0:n], m_, op=ALU.mult), [rw_, rsbm], [rwm])
                        else:
                            wm, rwm = w_, rw_
                        wms[k] = (wm, rwm)

                    def stage2b(b, k):
                        wm, rwm = wms.pop(k)
                        mm(po[:, 0:n], vt[:, b, :], wm[:, 0:n], k == 0, k == nb - 1, [rvt, rwm], [rpo])

                    nb = len(blocks)
                    stage1(blocks[0], 0)
                    pw, rpw = next_ps()
                    for _ in range(WARM_N):
                        mm(pw[:, 0:512], kt[:, 0:128], kt[:, 0:512], True, True, [rkt], [rpw])
                    if nb > 1:
                        stage1(blocks[1], 1)
                    stage2a(blocks[0], 0)
                    for k in range(nb):
                        if k + 2 < nb:
                            stage1(blocks[k + 2], k + 2)
                        if k + 1 < nb:
                            stage2a(blocks[k + 1], k + 1)
                        stage2b(blocks[k], k)
                        pump()
                    evac(o_sb[:, h, i0:i0 + n], po[:, 0:n], [rpo], [ro_sb])
            for _ in selg:
                pass
        fw.barrier()

        o_ds, ro_ds = sbt(LS, "o_ds", [128, 8, NQ], BF16)
        with scope(RS) as c3:
            ebt, rebt = sbt(c3, "ebt", [128, 8, TW], BF16)
            rbt, rrbt = sbt(c3, "rbt", [128, 8], F32)
            fw.dma(rbt, rb15, writes=[rrbt])
            for h in range(8):
                fw.dma(ebt[:, h, :], tbias[:, h, :], writes=[rebt], q="pool")
            selTb = [sbt(c3, f"selT{i}", [128, 32, 342], BF16) for i in range(2)]
            kb = [sbt(c3, f"dkb{i}", [128, 4096], BF16) for i in range(2)]
            vb = [sbt(c3, f"dvb{i}", [128, 32, 128], BF16) for i in range(2)]
            qb = [sbt(c3, f"dqb{i}", [128, 342], BF16) for i in range(2)]
            pmb = [sbt(c3, f"dpm{i}", [128, 342], BF16) for i in range(4)]
            osb = [sbt(c3, f"dos{i}", [128, 342], BF16) for i in range(2)]
            rdb = [sbt(c3, f"rden{i}", [128, 342], F32) for i in range(2)]
            o32b = [sbt(c3, f"o32{i}", [128, 342], F32) for i in range(2)]
            hcount = [0]

            def emit_head(s, h):
                i0, n, tau_min, nblk, nks = geom(s)
                selT_, rselT_ = selTb[s % 2]
                b_near = max(0, (tau_min - 255) // 128 + 1)
                hc = hcount[0]
                hcount[0] += 1
                kt, rkt = kb[hc % 2]
                vt, rvt = vb[hc % 2]
                qt_, rqt = qb[hc % 2]
                fw.dma(kt[:, 0:128 * nblk], KT[1][h, :, 0:128 * nblk], writes=[rkt])
                fw.dma(vt[:, 0:nblk, :], VS[1].rearrange("(b p) f -> p b f", p=128)[:, 0:nblk, 128 * h:128 * h + 128], writes=[rvt])
                fw.dma(qt_[:, 0:n], QT[1][h, :, i0:i0 + n], writes=[rqt])
                po, rpo = pO[hc % 2]
                pd, rpd = (pgen4, pgen5)[hc % 2]
                pls = {}
                pms = {}

                def sA(b):
                    pl, rpl = next_ps()
                    near = b >= b_near
                    mm(pl[:, 0:n], kt[:, 128 * b:128 * b + 128], qt_[:, 0:n], True, False, [rkt, rqt], [rpl])
                    mm(pl[:, 0:n], ident, selT_[:, b, 0:n], False, not near, [rcm, rselT_], [rpl])
                    if near:
                        delta = 128 * b - OWN0 - i0
                        jj = J0 - delta
                        assert 0 <= jj and jj + n <= TW, (jj, n)
                        mm(pl[:, 0:n], ident, ebt[:, h, jj:jj + n], False, True, [rcm, rebt], [rpl])
                    pls[b] = (pl, rpl)

                def sB(b):
                    pl, rpl = pls.pop(b)
                    pm, rpm = pmb[b % 4]
                    if b < b_near:
                        actf(pm[:, 0:n], pl[:, 0:n], AF.Exp, [rpl, rrbt], [rpm], bias=rbt[:, h:h + 1], scale=1.0)
                    else:
                        actf(pm[:, 0:n], pl[:, 0:n], AF.Exp, [rpl], [rpm])
                    pms[b] = (pm, rpm)

                def sC(b):
                    pm, rpm = pms.pop(b)
                    mm(po[:, 0:n], vt[:, b, :], pm[:, 0:n], b == 0, b == nblk - 1, [rvt, rpm], [rpo])
                    mm(pd[:, 0:n], ones, pm[:, 0:n], b == 0, b == nblk - 1, [rcm, rpm], [rpd])

                sA(0)
                if nblk > 1:
                    sA(1)
                sB(0)
                for b in range(nblk):
                    if b + 2 < nblk:
                        sA(b + 2)
                    if b + 1 < nblk:
                        sB(b + 1)
                    sC(b)
                rd_, rrd_ = rdb[hc % 2]
                o32, ro32 = o32b[hc % 2]
                actf(rd_[:, 0:n], pd[:, 0:n], AF.Ln, [rpd, rcst], [rrd_], bias=cst[:, 3:4], scale=1.0)
                actf(rd_[:, 0:n], rd_[:, 0:n], AF.Exp, [rrd_], [rrd_], scale=-1.0)
                actf(o32[:, 0:n], po[:, 0:n], AF.Copy, [rpo], [ro32])
                fw.op(pool, lambda e: e.tensor_tensor(o_ds[:, h, i0:i0 + n], o32[:, 0:n], rd_[:, 0:n], op=ALU.mult), [ro32, rrd_], [ro_ds])

            for s in range(3):
                _, n_, _, nblk_, _ = geom(s)
                st_, rst_ = selTb[s % 2]
                fw.dma(st_[:, 0:nblk_, 0:n_], SELS[s][:, 0:nblk_, 0:n_], reads=[rSEL[s]], writes=[rst_])
                for h in range(8):
                    emit_head(s, h)
        fw.barrier()
        print("phase3 min SBUF gap bytes/partition:", gapmin[0])

        h3, rh3 = sbt(RS, "h3", [128, KC, NQ], BF16)
        mark_h3 = RS.top
        mT, rmT = sbt(RS, "mT", [128, KC, NQ], BF16)
        with scope(RS) as c4a:
            hO, rhO = sbt(c4a, "hO", [128, KC, NQ], BF16)
            with scope(RS) as c4n:
                xs, rxs = sbt(c4n, "xs4", [128, KC, 342], F32)
                csq = mk_sq(c4n, 342)
                for (i0, n) in QS:
                    fw.dma(xs[:, :, 0:n], xw[:, :, OWN0 + i0: OWN0 + i0 + n], writes=[rxs])
                    rms_core(csq, xs, rxs, n, 0, lambda kc: hO[:, kc, i0:i0 + n], [rhO])
            fw.barrier()
            wg = [sbt(c4a, f"wg{i}", [128, KC, 256], BF16) for i in range(3)]
            wp = [sbt(c4a, f"wp{i}", [128, 8, 256], BF16) for i in range(3)]
            bg, rbg = sbt(c4a, "bg", [128, 32], F32)
            fw.dma(bg[:], bgate, writes=[rbg])
            gsb = [sbt(c4a, f"gs{i}", [128, 342], F32) for i in range(4)]
            tsb = [sbt(c4a, f"ts{i}", [128, 342], F32) for i in range(4)]
            for nch in range(16):
                wgt, rwg = wg[nch % 3]
                wpt, rwp = wp[nch % 3]
                load_w(wgt[:, :, 0:128], rwg, w_in[:, :, 7248 + 128 * nch: 7248 + 128 * nch + 128], KC)
                load_w(wgt[:, :, 128:256], rwg, w_in[:, :, 9296 + 128 * nch: 9296 + 128 * nch + 128], KC)
                load_w(wpt[:, :, 0:128], rwp, w_psb[:, :, 128 * nch:128 * nch + 128], 8)
                load_w(wpt[:, :, 128:256], rwp, w_pds[:, :, 128 * nch:128 * nch + 128], 8)
                for si, (i0, n) in enumerate(QS):
                    ts_ = []
                    for br in range(2):
                        pgt, rpg = next_ps()
                        for kc in range(KC):
                            mm(pgt[:, 0:n], wgt[:, kc, 128 * br:128 * br + 128], hO[:, kc, i0:i0 + n], kc == 0, kc == KC - 1, [rwg, rhO], [rpg])
                        g_, rg_ = gsb[(si * 2 + br) % 4]
                        actf(g_[:, 0:n], pgt[:, 0:n], AF.Sigmoid, [rpg, rbg], [rg_], bias=bg[:, 16 * br + nch: 16 * br + nch + 1], scale=1.0)
                        pp, rpp = next_ps()
                        osrc, rosrc = (o_sb, ro_sb) if br == 0 else (o_ds, ro_ds)
                        for kc in range(8):
                            mm(pp[:, 0:n], wpt[:, kc, 128 * br:128 * br + 128], osrc[:, kc, i0:i0 + n], kc == 0, kc == 7, [rwp, rosrc], [rpp])
                        t_, rt_ = tsb[(si * 2 + br) % 4]
                        fw.op(dve, lambda e: e.tensor_tensor(t_[:, 0:n], pp[:, 0:n], g_[:, 0:n], op=ALU.mult), [rpp, rg_], [rt_])
                        ts_.append((t_, rt_))
                    fw.op(dve, lambda e: e.tensor_tensor(mT[:, nch, i0:i0 + n], ts_[0][0][:, 0:n], ts_[1][0][:, 0:n], op=ALU.add),
                          [ts_[0][1], ts_[1][1]], [rmT])
        fw.barrier()
        LS.top = L_consts
        x1, rx1 = sbt(LS, "x1", [128, KC, NQ], F32)
        with scope(RS) as c4b:
            wo = [sbt(c4b, f"wo{i}", [128, KC, 128], BF16) for i in range(4)]
            xi = [sbt(c4b, f"xi{i}", [128, NQ], F32) for i in range(2)]
            for nch in range(16):
                wot, rwo = wo[nch % 4]
                load_w(wot, rwo, w_out[:, :, 128 * nch:128 * nch + 128], KC)
                xt, rxt = xi[nch % 2]
                fw.dma(xt[:, 0:NQ], xw[:, nch, OWN0:OWN0 + NQ], writes=[rxt])
                for (i0, n) in QS:
                    p, rp = next_ps()
                    for kc in range(KC):
                        mm(p[:, 0:n], wot[:, kc, :], mT[:, kc, i0:i0 + n], kc == 0, kc == KC - 1, [rwo, rmT], [rp])
                    fw.op(dve, lambda e: e.tensor_tensor(x1[:, nch, i0:i0 + n], p[:, 0:n], xt[:, i0:i0 + n], op=ALU.add), [rp, rxt], [rx1])
        fw.barrier()

        hq, rhq = mT, rmT
        with scope(RS) as c5:
            hm, rhm = sbt(c5, "hm", [128, KC, 256], BF16)
            qc, rqc = sbt(c5, "qc", [128, 4, NQ], BF16)
            km, rkm = sbt(c5, "km", [128, 4, 256], BF16)
            vm, rvm = sbt(c5, "vm", [128, 2, 512], BF16)
            oc, roc = sbt(c5, "oc", [128, 4, NQ], BF16)
            wt, rwt = sbt(c5, "w5", [128, KC, 512], BF16)
            load_w(wt, rwt, w_cq, KC)
            with scope(RS) as c5n:
                csq = mk_sq(c5n, 342)
                for (i0, n) in QS:
                    rms_core(csq, x1[:, :, i0:i0 + n], rx1, n, 1, lambda kc: hq[:, kc, i0:i0 + n], [rhq])
                xm, rxm = sbt(c5n, "xm", [128, KC, 256], F32)
                fw.dma(xm, memT, writes=[rxm])
                rms_core(csq, xm, rxm, 256, 2, lambda kc: hm[:, kc, :], [rhm])
            fw.barrier()
            wt2, rwt2 = sbt(c5, "w5b", [128, KC, 512], BF16)
            load_w(wt2, rwt2, w_ckv[:, :, 0:512], KC)
            for h in range(4):
                for (i0, n) in QS:
                    p, rp = next_ps()
                    for kc in range(KC):
                        mm(p[:, 0:n], wt[:, kc, 128 * h:128 * h + 128], hq[:, kc, i0:i0 + n], kc == 0, kc == KC - 1, [rwt, rhq], [rp])
                    evac(qc[:, h, i0:i0 + n], p[:, 0:n], [rp], [rqc], scale=128 ** -0.5)
            load_w(wt, rwt, w_ckv[:, :, 512:1024], KC)
            for h in range(4):
                p, rp = next_ps()
                for kc in range(KC):
                    mm(p[:, 0:256], wt2[:, kc, 128 * h:128 * h + 128], hm[:, kc, :], kc == 0, kc == KC - 1, [rwt2, rhm], [rp])
                evac(km[:, h, :], p[:, 0:256], [rp], [rkm])
            for mt in range(2):
                p, rp = next_ps()
                for kc in range(KC):
                    mm(p[:, 0:512], hm[:, kc, 128 * mt:128 * mt + 128], wt[:, kc, :], kc == 0, kc == KC - 1, [rwt, rhm], [rp])
                evac(vm[:, mt, :], p[:, 0:512], [rp], [rvm])
            pcb = [sbt(c5, f"pc{i}", [128, 342], BF16) for i in range(3)]
            rdc, rrdc = sbt(c5, "rdc", [128, 342], F32)
            kk = 0
            for h in range(4):
                for (i0, n) in QS:
                    po, rpo = pO[0]
                    pd, rpd = pO[1]
                    for mt in range(2):
                        pl, rpl = next_ps()
                        mm(pl[:, 0:n], km[:, h, 128 * mt:128 * mt + 128], qc[:, h, i0:i0 + n], True, True, [rkm, rqc], [rpl])
                        pc_, rpc = pcb[kk % 3]
                        kk += 1
                        actf(pc_[:, 0:n], pl[:, 0:n], AF.Exp, [rpl], [rpc])
                        mm(po[:, 0:n], vm[:, mt, 128 * h:128 * h + 128], pc_[:, 0:n], mt == 0, mt == 1, [rvm, rpc], [rpo])
                        mm(pd[:, 0:n], ones, pc_[:, 0:n], mt == 0, mt == 1, [rcm, rpc], [rpd])
                    fw.op(dve, lambda e: e.reciprocal(rdc[:, 0:n], pd[:, 0:n]), [rpd], [rrdc])
                    fw.op(dve, lambda e: e.tensor_tensor(oc[:, h, i0:i0 + n], po[:, 0:n], rdc[:, 0:n], op=ALU.mult), [rpo, rrdc], [roc])
            wc = [sbt(c5, f"wc{i}", [128, 4, 128], BF16) for i in range(2)]
            for nch in range(16):
                wct, rwc = wc[nch % 2]
                load_w(wct, rwc, w_co[:, :, 128 * nch:128 * nch + 128], 4)
                for (i0, n) in QS:
                    p, rp = next_ps()
                    for kc in range(4):
                        mm(p[:, 0:n], wct[:, kc, :], oc[:, kc, i0:i0 + n], kc == 0, kc == 3, [rwc, roc], [rp])
                    fw.op(dve, lambda e: e.tensor_tensor(x1[:, nch, i0:i0 + n], p[:, 0:n], x1[:, nch, i0:i0 + n], op=ALU.add), [rp, rx1], [rx1])
        fw.barrier()
        with scope(RS) as c6n:
            csq = mk_sq(c6n, 342)
            for (i0, n) in QS:
                rms_core(csq, x1[:, :, i0:i0 + n], rx1, n, 3, lambda kc: h3[:, kc, i0:i0 + n], [rh3])
        rx2s = Res()
        for kc in range(0, KC, 4):
            fw.dma(X2S[:, kc:kc + 4, :], x1[:, kc:kc + 4, :], reads=[rx1], writes=[rx2s])
        fw.barrier()
        RS.top = mark_h3
        LS.top = L_consts

        gat, rgat = sbt(LS, "gat", [128, 48, 1024], BF16)
        with scope(RS) as c6a:
            cw, rcw = sbt(c6a, "cw", [128, 3, 96], F32)
            cb, rcb = sbt(c6a, "cb", [128, 96], F32)
            fw.dma(cw, convw, writes=[rcw])
            fw.dma(cb, convb, writes=[rcb])
            wu = [sbt(c6a, f"wu{i}", [128, KC, 128], BF16) for i in range(6)]
            ub = [sbt(c6a, f"u{i}", [128, NQ], F32) for i in range(2)]
            cbuf = [sbt(c6a, f"c{i}", [128, 1024], F32) for i in range(2)]
            ga, rga = sbt(c6a, "ga", [128, 1024], F32)
            wk = 0
            for i in range(48):
                cs = []
                for part in range(2):
                    ch = i + 48 * part
                    wt, rwt = wu[wk % 6]
                    wk += 1
                    load_w(wt, rwt, w_up[:, :, 128 * ch:128 * ch + 128], KC)
                    u_, ru = ub[part]
                    for (i0, n) in QS:
                        p, rp = next_ps()
                        for kc in range(KC):
                            mm(p[:, 0:n], wt[:, kc, :], h3[:, kc, i0:i0 + n], kc == 0, kc == KC - 1, [rwt, rh3], [rp])
                        actf(u_[:, i0:i0 + n], p[:, 0:n], AF.Copy, [rp], [ru])
                    fw.op(dve, lambda e: e.tensor_scalar(u_[:, 0:2], u_[:, 0:2], hvt[:, 0:1], None, op0=ALU.mult), [ru, rhv], [ru])
                    c_, rc_ = cbuf[part]
                    fw.op(dve, lambda e: e.tensor_scalar(c_[:, 0:1024], u_[:, 2:1026], cw[:, 2, ch:ch + 1], cb[:, ch:ch + 1],
                                                         op0=ALU.mult, op1=ALU.add), [ru, rcw, rcb], [rc_])
                    fw.op(dve, lambda e: e.scalar_tensor_tensor(c_[:, 0:1024], u_[:, 1:1025], cw[:, 1, ch:ch + 1], c_[:, 0:1024],
                                                                op0=ALU.mult, op1=ALU.add), [ru, rcw, rc_], [rc_])
                    fw.op(dve, lambda e: e.scalar_tensor_tensor(c_[:, 0:1024], u_[:, 0:1024], cw[:, 0, ch:ch + 1], c_[:, 0:1024],
                                                                op0=ALU.mult, op1=ALU.add), [ru, rcw, rc_], [rc_])
                    cs.append((c_, rc_))
                actf(ga[:, 0:1024], cs[0][0][:, 0:1024], AF.Gelu_apprx_tanh, [cs[0][1]], [rga])
                fw.op(dve, lambda e: e.tensor_tensor(gat[:, i, :], ga[:, 0:1024], cs[1][0][:, 0:1024], op=ALU.mult), [rga, cs[1][1]], [rgat])
        fw.barrier()
        RS.top = ARENA * 4
        with scope(RS) as c7:
            wd = [sbt(c7, f"wd{i}", [128, 48, 128], BF16) for i in range(3)]
            xi = [sbt(c7, f"x2i{i}", [128, 1024], F32) for i in range(2)]
            x3c = [sbt(c7, f"x3c{i}", [128, 1024], F32) for i in range(3)]
            sqc = [sbt(c7, f"sqc{i}", [128, 512], BF16) for i in range(3)]
            lnf, rlnf = sbt(c7, "lnf", [128, 1024], F32)
            rsf, rrsf = sbt(c7, "rsf", [128, 1024], F32)
            ost = [sbt(c7, f"ost{i}", [128, 1024], F32) for i in range(2)]
            pss = [pgen4, pgen5]
            rx3s = [Res() for _ in range(16)]
            sqi = 0
            for nch in range(16):
                wdt, rwd = wd[nch % 3]
                for a in range(0, 48, 12):
                    fw.dma(wdt[:, a:a + 12, :], w_down[:, a:a + 12, 128 * nch:128 * nch + 128], writes=[rwd], q="pool")
                xt, rxt = xi[nch % 2]
                fw.dma(xt, X2S[:, nch, 2:1026], reads=[rx2s], writes=[rxt])
                x3_, rx3_ = x3c[nch % 3]
                for sl in range(2):
                    a0 = 512 * sl
                    p, rp = next_ps()
                    for kc in range(48):
                        mm(p[:, 0:512], wdt[:, kc, :], gat[:, kc, a0:a0 + 512], kc == 0, kc == 47, [rwd, rgat], [rp])
                    fw.op(dve, lambda e: e.tensor_tensor(x3_[:, a0:a0 + 512], p[:, 0:512], xt[:, a0:a0 + 512], op=ALU.add), [rp, rxt], [rx3_])
                    sq_, rsq_ = sqc[sqi % 3]
                    sqi += 1
                    actf(sq_, x3_[:, a0:a0 + 512], AF.Square, [rx3_], [rsq_])
                    ps_, rps_ = pss[sl]
                    mm(ps_[:, 0:512], ones, sq_, nch == 0, nch == 15, [rsq_, rcm], [rps_])
                fw.dma(X3S[:, nch, :], x3_, reads=[rx3_], writes=[rx3s[nch]], q="act")
            for sl in range(2):
                ps_, rps_ = pss[sl]
                actf(lnf[:, 512 * sl:512 * sl + 512], ps_[:, 0:512], AF.Ln, [rps_, rcst], [rlnf], bias=cst[:, 0:1], scale=1.0 / D)
            actf(rsf, lnf, AF.Exp, [rlnf], [rrsf], scale=-0.5)
            for nch in range(16):
                x3_, rx3_ = x3c[nch % 3]
                fw.dma(x3_, X3S[:, nch, :], reads=[rx3s[nch]], writes=[rx3_])
                o_, ro_ = ost[nch % 2]
                fw.op(dve, lambda e: e.scalar_tensor_tensor(o_, x3_, gv[:, 4, nch:nch + 1], rsf, op0=ALU.mult, op1=ALU.mult),
                      [rx3_, rrsf, rgv], [ro_])
                fw.dma(outT[:, nch, :], o_, reads=[ro_], q="pool")
        fw.finish()
        print("instructions emitted:", fw.n_inst)
    return nc


def _bucket(rel):
    rel = np.asarray(rel, np.int64)
    nb = 16
    max_exact = 8
    ret = np.where(rel > 0, nb, 0)
    n = np.abs(rel)
    nf = np.maximum(n, 1).astype(np.float32)
    lg = (np.log(nf / np.float32(max_exact)).astype(np.float32) / np.float32(np.log(128 / 8))).astype(np.float32)
    large = max_exact + (lg * np.float32(nb - max_exact)).astype(np.int32)
    large = np.minimum(large, nb - 1)
    return ret + np.where(n < max_exact, n, large)


def _fm(v, n):
    return np.ascontiguousarray(np.asarray(v, np.float32).reshape(n, 128).T)


_NC_CACHE = {}


def kernel(x, mem, g_mix, w_in, b_gate, w_proj_sb, w_proj_dsa, w_out, rel_bias, g_cross, g_mem,
           w_cq, w_ckv, w_co, g_ffn, w_up, conv_w, conv_b, w_down, g_final):
    f32 = np.float32
    x = np.asarray(x, f32)
    mem = np.asarray(mem, f32)
    rel_bias = np.asarray(rel_bias, f32)
    if "nc" not in _NC_CACHE:
        _NC_CACHE["nc"] = build()
    nc = _NC_CACHE["nc"]

    shared = {
        "w_in": np.ascontiguousarray(np.asarray(w_in, f32)[0]),
        "w_proj_sb": np.ascontiguousarray(np.asarray(w_proj_sb, f32)[0]),
        "w_proj_dsa": np.ascontiguousarray(np.asarray(w_proj_dsa, f32)[0]),
        "w_out": np.ascontiguousarray(np.asarray(w_out, f32)[0]),
        "w_cq": np.ascontiguousarray(np.asarray(w_cq, f32)[0]),
        "w_ckv": np.ascontiguousarray(np.asarray(w_ckv, f32)[0]),
        "w_co": np.ascontiguousarray(np.asarray(w_co, f32)[0]),
        "w_up": np.ascontiguousarray(np.asarray(w_up, f32)[0]),
        "w_down": np.ascontiguousarray(np.asarray(w_down, f32)[0]),
    }
    gs = np.stack([_fm(np.asarray(g, f32).reshape(-1), 16) for g in (g_mix, g_cross, g_mem, g_ffn, g_final)], axis=1)
    shared["gvec"] = np.ascontiguousarray(gs)
    shared["bgate"] = _fm(np.asarray(b_gate, f32).reshape(-1), 32)
    cwv = np.asarray(conv_w, f32)[0]
    shared["convw"] = np.ascontiguousarray(np.stack([_fm(cwv[i], 96) for i in range(3)], axis=1))
    shared["convb"] = _fm(np.asarray(conv_b, f32).reshape(-1), 96)
    p = np.arange(128)[:, None]
    jj = np.arange(TW)[None, :]
    bk = _bucket(p - (jj - J0))
    shared["tbias"] = np.ascontiguousarray(np.transpose(rel_bias[bk], (0, 2, 1)))
    shared["rb15"] = np.ascontiguousarray(np.broadcast_to(rel_bias[15][None, :], (128, 8)))
    sbm = np.zeros((128, 11, 342), f32)
    mi = 0
    for s, (i0, n) in enumerate(QS):
        b_hi = (OWN0 + i0 + n - 2) // 128
        b_full = (OWN0 + i0 - 128) // 128
        for b in range(b_full + 1, b_hi + 1):
            ks = 128 * b + np.arange(128)[:, None]
            qs = OWN0 + i0 + np.arange(342)[None, :]
            sbm[:, mi, :] = (ks < qs).astype(f32)
            mi += 1
    shared["sbm"] = sbm
    cm = np.zeros((128, 4, 128), f32)
    jx = np.arange(128)[:, None]
    sx = np.arange(128)[None, :]
    cm[:, 0, :] = -(jx >= sx).astype(f32)
    cm[:, 1, :] = -1.0
    cm[:, 2, :] = 1.0
    cm[:, 3, :] = np.eye(128, dtype=f32)
    shared["cmat"] = cm
    shared["ckv"] = np.ascontiguousarray(np.broadcast_to((2.0 ** -(np.arange(NIT) + 1.0)).astype(f32)[None, :], (128, NIT)))

    in_maps = []
    for c in range(8):
        b, j = c // 4, c % 4
        t0 = j * 1024
        win = np.zeros((4096, D), f32)
        lo = t0 - 3072
        src0 = max(lo, 0)
        win[src0 - lo:, :] = x[b, src0:t0 + 1024, :]
        xwin = np.ascontiguousarray(win.T.reshape(16, 128, 4096).transpose(1, 0, 2))
        mT = np.ascontiguousarray(mem[b].T.reshape(16, 128, 256).transpose(1, 0, 2))
        tq = t0 - 2 + np.arange(NQ)[:, None]
        tk = lo + np.arange(4096)[None, :]
        vis = (tk >= 0) & ((tk // 64) <= (tq // 64))
        dm = np.where(vis, 0.0, -1e30).astype(f32)
        m = dict(shared)
        m["xw"] = xwin
        m["memT"] = mT
        m["dmask"] = np.ascontiguousarray(dm)
        m["hv"] = np.full((128, 1), 1.0 if j > 0 else 0.0, f32)
        in_maps.append(m)

    res = run_bass_kernel_spmd(nc, in_maps, core_ids=list(range(8)))
    out = np.zeros((2, 4096, D), f32)
    for c in range(8):
        b, j = c // 4, c % 4
        o = np.asarray(res.results[c]["outT"], f32)
        out[b, j * 1024:(j + 1) * 1024, :] = o.transpose(2, 1, 0).reshape(1024, D)
    if DEBUG:
        kernel.last = res
    return out
```

```python
import numpy as np
from contextlib import ExitStack, contextmanager
import concourse.bass as bass
import concourse.mybir as mybir
from concourse.bass_utils import run_bass_kernel_spmd

F32 = mybir.dt.float32
BF16 = mybir.dt.bfloat16
AF = mybir.ActivationFunctionType
ALU = mybir.AluOpType
AX = mybir.AxisListType

D = 2048
KC = 16
NQ = 1026
OWN0 = 3070
QS = [(0, 342), (342, 342), (684, 342)]
EPS = 1e-6
NIT = 24
TW = 1000
J0 = 404
DFF = 6144
DEBUG = False
WARM_N = 16


class Res:
    __slots__ = ("w", "r")

    def __init__(self):
        self.w = None
        self.r = []


class Eng:
    def __init__(self, name, obj, sem, is_pe=False):
        self.name = name
        self.obj = obj
        self.sem = sem
        self.count = 0
        self.waited = {}
        self.is_pe = is_pe


class FW:
    def __init__(self, nc, ctx, n_dsem=12):
        self.nc = nc
        mk = lambda n: ctx.enter_context(nc.semaphore(n))
        self.pe = Eng("pe", nc.tensor, mk("s_pe"), is_pe=True)
        self.act = Eng("act", nc.scalar, mk("s_act"))
        self.dve = Eng("dve", nc.vector, mk("s_dve"))
        self.pool = Eng("pool", nc.gpsimd, mk("s_pool"))
        self.sp = Eng("sp", nc.sync, mk("s_sp"))
        self.engs = [self.pe, self.act, self.dve, self.pool, self.sp]
        self.dsems = {}
        for q in ("sp", "pool", "act"):
            self.dsems[q] = [[mk(f"d_{q}{i}"), 0] for i in range(n_dsem if q != "act" else 4)]
        self.dma_i = {"sp": 0, "pool": 0, "act": 0}
        self.n_inst = 0

    def _wait(self, eng, tok):
        if tok is None:
            return
        sem, val, owner = tok
        if owner is eng and eng.is_pe:
            return
        key = id(sem)
        if eng.waited.get(key, 0) >= val:
            return
        eng.obj.wait_ge(sem, val)
        eng.waited[key] = val

    def _deps(self, eng, reads, writes):
        for r in reads:
            self._wait(eng, r.w)
        for w in writes:
            self._wait(eng, w.w)
            for t in w.r:
                self._wait(eng, t)

    def _record(self, tok, reads, writes):
        for r in reads:
            r.r.append(tok)
            if len(r.r) > 24:
                best = {}
                for t in r.r:
                    k = id(t[0])
                    if k not in best or best[k][1] < t[1]:
                        best[k] = t
                r.r = list(best.values())
        for w in writes:
            w.w = tok
            w.r = []

    def op(self, eng, fn, reads=(), writes=()):
        self._deps(eng, reads, writes)
        inst = fn(eng.obj)
        eng.count += 1
        inst.then_inc(eng.sem, 1)
        tok = (eng.sem, eng.count, eng)
        self._record(tok, reads, writes)
        self.n_inst += 1
        return tok

    def dma(self, out, in_, reads=(), writes=(), q="sp"):
        eng = {"sp": self.sp, "pool": self.pool, "act": self.act}[q]
        slots = self.dsems[q]
        i = self.dma_i[q]
        self.dma_i[q] = i + 1
        slot = slots[i % len(slots)]
        if slot[1] > 0:
            self._wait(eng, (slot[0], slot[1], None))
        self._deps(eng, reads, writes)
        inst = eng.obj.dma_start(out=out, in_=in_)
        slot[1] += 16
        inst.then_inc(slot[0], 16)
        tok = (slot[0], slot[1], None)
        self._record(tok, reads, writes)
        self.n_inst += 1
        return tok

    def all_tokens(self):
        toks = []
        for e in self.engs:
            if e.count > 0:
                toks.append((e.sem, e.count, e))
        for q in self.dsems:
            for s in self.dsems[q]:
                if s[1] > 0:
                    toks.append((s[0], s[1], None))
        return toks

    def barrier(self):
        toks = self.all_tokens()
        for e in self.engs:
            for t in toks:
                if t[2] is e:
                    continue
                self._wait(e, t)

    def finish(self):
        for t in self.all_tokens():
            self._wait(self.sp, t)


def build():
    nc = bass.Bass("TRN2", target_bir_lowering=False)

    def din(name, shape, dtype=F32):
        return nc.dram_tensor(name, shape, dtype, kind="ExternalInput").ap()

    xw = din("xw", [128, KC, 4096])
    memT = din("memT", [128, KC, 256])
    w_in = din("w_in", [D, 11344]).rearrange("(kc p) n -> p kc n", p=128)
    w_psb = din("w_proj_sb", [1024, D]).rearrange("(kc p) n -> p kc n", p=128)
    w_pds = din("w_proj_dsa", [1024, D]).rearrange("(kc p) n -> p kc n", p=128)
    w_out = din("w_out", [D, D]).rearrange("(kc p) n -> p kc n", p=128)
    w_cq = din("w_cq", [D, 512]).rearrange("(kc p) n -> p kc n", p=128)
    w_ckv = din("w_ckv", [D, 1024]).rearrange("(kc p) n -> p kc n", p=128)
    w_co = din("w_co", [512, D]).rearrange("(kc p) n -> p kc n", p=128)
    w_up = din("w_up", [D, 2 * DFF]).rearrange("(kc p) n -> p kc n", p=128)
    w_down = din("w_down", [DFF, D]).rearrange("(kc p) n -> p kc n", p=128)
    gvec = din("gvec", [128, 5, KC])
    bgate = din("bgate", [128, 32])
    convw = din("convw", [128, 3, 96])
    convb = din("convb", [128, 96])
    dmask = din("dmask", [NQ, 4096])
    tbias = din("tbias", [128, 8, TW])
    rb15 = din("rb15", [128, 8])
    sbm_d = din("sbm", [128, 11, 342])
    cmat = din("cmat", [128, 4, 128])
    hv_d = din("hv", [128, 1])
    ckv_d = din("ckv", [128, NIT])
    outT = nc.dram_tensor("outT", [128, KC, 1024], F32, kind="ExternalOutput").ap()

    skind = "ExternalOutput" if DEBUG else "Internal"
    KT = [nc.dram_tensor(f"KT{i}", [8, 128, 4096], BF16, kind=skind).ap() for i in range(2)]
    VS = [nc.dram_tensor(f"VS{i}", [4096, 1024], BF16, kind=skind).ap() for i in range(2)]
    QT = [nc.dram_tensor(f"QT{i}", [8, 128, NQ], BF16, kind=skind).ap() for i in range(3)]
    X2S = nc.dram_tensor("X2S", [128, KC, NQ], F32, kind=skind).ap()
    SELS = [nc.dram_tensor(f"SELS{i}", [128, 32, 342], BF16, kind=skind).ap() for i in range(3)]
    X3S = nc.dram_tensor("X3S", [128, KC, 1024], F32, kind=skind).ap()
    OSC = [nc.dram_tensor(f"OSC{i}", [128, 8, NQ], BF16, kind=skind).ap() for i in range(2)]
    if DEBUG:
        DBG = nc.dram_tensor("DBG", [128, 16, NQ], F32, kind="ExternalOutput").ap()

    with ExitStack() as ctx:
        fw = FW(nc, ctx)
        pe, act, dve, pool = fw.pe, fw.act, fw.dve, fw.pool

        ARENA = 52800
        big = ctx.enter_context(nc.sbuf_tensor("arena", [128, ARENA], F32))

        class Stk:
            def __init__(self, left):
                self.left = left
                self.top = 0 if left else ARENA * 4

            def alloc(self, shape, dtype):
                nel = 1
                for d_ in shape[1:]:
                    nel *= d_
                nb = nel * (2 if dtype == BF16 else 4)
                nb = (nb + 63) // 64 * 64
                if self.left:
                    off = self.top
                    self.top += nb
                else:
                    self.top -= nb
                    off = self.top
                assert LS.top <= RS.top, ("SBUF arena overflow", LS.top, RS.top)
                gapmin[0] = min(gapmin[0], RS.top - LS.top)
                v = big[:, off // 4: (off + nb) // 4]
                if dtype == BF16:
                    v = v.bitcast(BF16)
                v = v[:, 0:nel]
                if len(shape) == 3:
                    v = v.rearrange("p (a b) -> p a b", b=shape[2])
                return v

        gapmin = [1 << 30]
        LS = Stk(True)
        RS = Stk(False)

        @contextmanager
        def scope(stk):
            m_ = stk.top
            yield stk
            stk.top = m_

        def sbt(c, name, shape, dtype):
            return c.alloc(shape, dtype), Res()

        def pst(name, shape, dtype):
            return ctx.enter_context(nc.psum_tensor(name, shape, dtype)), Res()

        pgen = [pst(f"pg{i}", [128, 512], F32) for i in range(3)]
        pgen4 = pst("pg4", [128, 512], F32)
        pgen5 = pst("pg5", [128, 512], F32)
        pO = [pst(f"po{i}", [128, 512], F32) for i in range(2)]
        pS, rpS = pgen4
        pT, rpT = pst("ptr", [128, 1024], BF16)
        pgi = [0]

        def next_ps():
            p = pgen[pgi[0] % 3]
            pgi[0] += 1
            return p

        cm, rcm = sbt(LS, "cm", [128, 4, 128], BF16)
        gv, rgv = sbt(LS, "gv", [128, 5, KC], F32)
        cst, rcst = sbt(LS, "cst", [128, 4], F32)
        hvt, rhv = sbt(LS, "hvt", [128, 1], F32)
        L_consts = LS.top
        Ltri, nones, ones, ident = cm[:, 0, :], cm[:, 1, :], cm[:, 2, :], cm[:, 3, :]
        fw.dma(cm[:], cmat, writes=[rcm], q="pool")
        fw.dma(gv[:], gvec, writes=[rgv])
        fw.dma(hvt[:], hv_d, writes=[rhv])
        fw.op(dve, lambda e: e.memset(cst[:, 0:1], EPS), writes=[rcst])
        fw.op(dve, lambda e: e.memset(cst[:, 1:2], 1.0), writes=[rcst])
        fw.op(dve, lambda e: e.memset(cst[:, 2:3], -30000.0), writes=[rcst])
        fw.op(dve, lambda e: e.memset(cst[:, 3:4], 1e-30), writes=[rcst])

        def mm(out, lhsT, rhs, start, stop, reads, writes):
            return fw.op(pe, lambda e: e.matmul(out, lhsT, rhs, start=start, stop=stop), reads, writes)

        def actf(out, in_, func, reads, writes, **kw):
            return fw.op(act, lambda e: e.activation(out, in_, func, **kw), reads, writes)

        evi = [0]

        def evac(dst, src, reads, writes, scale=None):
            evi[0] += 1
            if evi[0] % 2 == 0:
                if scale is None:
                    return actf(dst, src, AF.Copy, reads, writes)
                return actf(dst, src, AF.Copy, reads, writes, scale=float(scale))
            if scale is None:
                return fw.op(dve, lambda e: e.tensor_copy(dst, src), reads, writes)
            return fw.op(dve, lambda e: e.tensor_scalar(dst, src, float(scale), None, op0=ALU.mult), reads, writes)

        def load_w(dst, rdst, src, nkc):
            h = max(1, nkc // 2)
            for a in range(0, nkc, h):
                fw.dma(dst[:, a:a + h, :], src[:, a:a + h, :], writes=[rdst], q="pool")

        def rms_core(c_sq, xs, rxs, ntok, gi, dst_fn, dst_res):
            sq, rsq, lnv, rln, rs, rrs = c_sq
            actf(sq[:, :, 0:ntok], xs[:, :, 0:ntok], AF.Square, [rxs], [rsq])
            for kc in range(KC):
                mm(pS[:, 0:ntok], ones, sq[:, kc, 0:ntok], kc == 0, kc == KC - 1, [rsq, rcm], [rpS])
            actf(lnv[:, 0:ntok], pS[:, 0:ntok], AF.Ln, [rpS, rcst], [rln], bias=cst[:, 0:1], scale=1.0 / D)
            actf(rs[:, 0:ntok], lnv[:, 0:ntok], AF.Exp, [rln], [rrs], scale=-0.5)
            for kc in range(KC):
                fw.op(dve, lambda e: e.scalar_tensor_tensor(dst_fn(kc), xs[:, kc, 0:ntok], gv[:, gi, kc:kc + 1],
                                                            rs[:, 0:ntok], op0=ALU.mult, op1=ALU.mult),
                      [rxs, rrs, rgv], dst_res)

        def mk_sq(c, w):
            sq, rsq = sbt(c, "sq", [128, KC, w], BF16)
            lnv, rln = sbt(c, "lnv", [128, w], F32)
            rs, rrs = sbt(c, "rs", [128, w], F32)
            return (sq, rsq, lnv, rln, rs, rrs)

        kix, rkix = sbt(LS, "kix", [128, 4096], BF16)
        wabs, rwabs = sbt(LS, "wabs", [128, 9, 16], F32)
        wsgn, rwsgn = sbt(LS, "wsgn", [128, 9, 16], F32)
        QT_tiles = []
        for s, (i0, n) in enumerate(QS):
            for m, (a, nq) in enumerate([(0, 128), (128, 128), (256, 86)]):
                QT_tiles.append((s, i0 + a, nq))

        with scope(RS) as c1:
            hT, _ = sbt(c1, "hT", [128, KC, 2048], BF16)
            hres = [Res() for _ in range(8)]
            xsb = [sbt(c1, f"xs{i}", [128, KC, 256], F32) for i in range(2)]
            csq = mk_sq(c1, 256)
            wts = [sbt(c1, f"wt{i}", [128, KC, 256], BF16) for i in range(4)]
            stg = [sbt(c1, f"stg{i}", [128, 512], BF16) for i in range(4)]
            wi = [0]
            si = [0]

            def nwt():
                w = wts[wi[0] % 4]
                wi[0] += 1
                return w

            def nstg():
                s_ = stg[si[0] % 4]
                si[0] += 1
                return s_

            def hr(a, b):
                return hres[a // 256:(b - 1) // 256 + 1]

            for g in range(2):
                for sl in range(8):
                    xs, rxs = xsb[sl % 2]
                    fw.dma(xs[:], xw[:, :, g * 2048 + sl * 256: g * 2048 + sl * 256 + 256], writes=[rxs])
                    rms_core(csq, xs, rxs, 256, 0, lambda kc: hT[:, kc, sl * 256: sl * 256 + 256], [hres[sl]])
                for br, (koff, voff) in enumerate([(1024, 2048), (4096, 5120)]):
                    for h in range(8):
                        wt, rwt = nwt()
                        load_w(wt[:, :, 0:128], rwt, w_in[:, :, koff + 128 * h: koff + 128 * h + 128], KC)
                        for s4 in range(4):
                            p, rp = next_ps()
                            for kc in range(KC):
                                mm(p[:, 0:512], wt[:, kc, 0:128], hT[:, kc, 512 * s4: 512 * s4 + 512], kc == 0, kc == KC - 1,
                                   [rwt] + hr(512 * s4, 512 * s4 + 512), [rp])
                            st, rst = nstg()
                            evac(st[:, 0:512], p[:, 0:512], [rp], [rst])
                            fw.dma(KT[br][h, :, g * 2048 + 512 * s4: g * 2048 + 512 * s4 + 512], st[:, 0:512], reads=[rst])
                    for cg in range(4):
                        wt, rwt = nwt()
                        load_w(wt[:, :, 0:256], rwt, w_in[:, :, voff + 256 * cg: voff + 256 * cg + 256], KC)
                        for blk in range(16):
                            p, rp = next_ps()
                            for kc in range(KC):
                                mm(p[:, 0:256], hT[:, kc, 128 * blk: 128 * blk + 128], wt[:, kc, 0:256], kc == 0, kc == KC - 1,
                                   [rwt] + hr(128 * blk, 128 * blk + 128), [rp])
                            st, rst = nstg()
                            evac(st[:, 0:256], p[:, 0:256], [rp], [rst])
                            r0 = g * 2048 + 128 * blk
                            fw.dma(VS[br][r0:r0 + 128, 256 * cg: 256 * cg + 256], st[:, 0:256], reads=[rst])
                wt, rwt = nwt()
                fw.dma(wt[:, :, 0:64], w_in[:, :, 7168:7232], writes=[rwt], q="pool")
                fw.dma(wt[:, :, 64:128], w_in[:, :, 7168:7232], writes=[rwt], q="pool")
                for s4 in range(4):
                    p, rp = next_ps()
                    for kc in range(KC):
                        mm(p[:, 0:512], wt[:, kc, 0:128], hT[:, kc, 512 * s4: 512 * s4 + 512], kc == 0, kc == KC - 1,
                           [rwt] + hr(512 * s4, 512 * s4 + 512), [rp])
                    evac(kix[:, g * 2048 + 512 * s4: g * 2048 + 512 * s4 + 512], p[:, 0:512], [rp], [rkix])
                if g == 1:
                    OB = 1022
                    for qi, (off, scale) in enumerate([(0, 128 ** -0.5), (3072, 128 ** -0.5), (6144, 0.125)]):
                        for h in range(8):
                            wt, rwt = nwt()
                            load_w(wt[:, :, 0:128], rwt, w_in[:, :, off + 128 * h: off + 128 * h + 128], KC)
                            for (i0, n) in QS:
                                p, rp = next_ps()
                                for kc in range(KC):
                                    mm(p[:, 0:n], wt[:, kc, 0:128], hT[:, kc, OB + i0: OB + i0 + n], kc == 0, kc == KC - 1,
                                       [rwt] + hr(OB + i0, OB + i0 + n), [rp])
                                st, rst = nstg()
                                evac(st[:, 0:n], p[:, 0:n], [rp], [rst], scale=scale)
                                fw.dma(QT[qi][h, :, i0:i0 + n], st[:, 0:n], reads=[rst])
                    wt, rwt = nwt()
                    fw.dma(wt[:, :, 0:16], w_in[:, :, 7232:7248], writes=[rwt], q="pool")
                    for qt, (s, iq, nq) in enumerate(QT_tiles):
                        p, rp = next_ps()
                        for kc in range(KC):
                            mm(p[0:nq, 0:16], hT[:, kc, OB + iq: OB + iq + nq], wt[:, kc, 0:16], kc == 0, kc == KC - 1,
                               [rwt] + hr(OB + iq, OB + iq + nq), [rp])
                        actf(wabs[0:nq, qt, :], p[0:nq, 0:16], AF.Abs, [rp], [rwabs], scale=0.25)
                        actf(wsgn[0:nq, qt, :], p[0:nq, 0:16], AF.Sign, [rp], [rwsgn])
        fw.barrier()

        L_kix = LS.top
        o_sb, ro_sb = sbt(LS, "o_sb", [128, 8, NQ], BF16)

        rSEL = [Res() for _ in range(3)]
        TILE_GEOM = [(0, 128), (128, 128), (256, 86)]

        def geom(s):
            i0, n = QS[s]
            tau_min = OWN0 + i0
            tau_max = OWN0 + i0 + n - 1
            vis_end = (tau_max // 64 + 1) * 64
            nblk = min((vis_end + 127) // 128, 32)
            nks = min((vis_end + 511) // 512, 8)
            return i0, n, tau_min, nblk, nks

        with scope(RS) as c2:
            sbm, rsbm = sbt(c2, "sbm", [128, 11, 342], BF16)
            fw.dma(sbm[:], sbm_d, writes=[rsbm], q="pool")
            kb = [sbt(c2, f"kb{i}", [128, 4096], BF16) for i in range(2)]
            vb = [sbt(c2, f"vb{i}", [128, 32, 128], BF16) for i in range(2)]
            qb = [sbt(c2, f"qb{i}", [128, NQ], BF16) for i in range(2)]
            eb = [sbt(c2, f"e{i}", [128, 342], F32) for i in range(3)]
            spb = [sbt(c2, f"sp{i}", [128, 342], BF16) for i in range(4)]
            spmb = [sbt(c2, f"spm{i}", [128, 342], BF16) for i in range(4)]
            wb_ = [sbt(c2, f"w{i}", [128, 342], BF16) for i in range(3)]
            wmb = [sbt(c2, f"wm{i}", [128, 342], BF16) for i in range(3)]
            accb = [sbt(c2, f"acc{i}", [128, 342], BF16) for i in range(3)]
            qixb = [sbt(c2, f"qix{i}", [128, 8, 342], BF16) for i in range(2)]
            ckt, rckt = sbt(c2, "ckt", [128, NIT], F32)
            fw.dma(ckt, ckv_d, writes=[rckt])
            accs2 = [sbt(c2, f"dacc{i}", [128, 4096], F32) for i in range(2)]
            dm16b = [sbt(c2, f"dm16{i}", [128, 4096], BF16) for i in range(2)]
            selqb = [sbt(c2, f"selq{i}", [128, 4096], BF16) for i in range(2)]
            selT, rselT = sbt(c2, "selTs", [128, 32, 342], BF16)
            rb_ = [sbt(c2, f"dr{i}", [128, 512], BF16) for i in range(4)]
            dgb = [sbt(c2, f"dg{i}", [128, 16, 128], BF16) for i in range(2)]
            sm, rsm = sbt(c2, "dsm", [128, 8], F32)
            wall, rwall = sbt(c2, "wall", [128, NIT], F32)

            def sel_gen():
                tiles = [(s_, m_) for s_ in range(3) for m_ in range(3)]

                def build_dg(qt2, hhs=range(16)):
                    nq2 = TILE_GEOM[qt2 % 3][1]
                    dg2, rdg2 = dgb[qt2 % 2]
                    for hh in hhs:
                        fw.op(pool, lambda e: e.tensor_scalar(dg2[0:nq2, hh, 0:nq2], ident[0:nq2, 0:nq2], wsgn[0:nq2, qt2, hh:hh + 1], None, op0=ALU.mult),
                              [rcm, rwsgn], [rdg2])

                def load_qix(s2):
                    i02, n2 = QS[s2]
                    qx, rqx = qixb[s2 % 2]
                    for hp in range(8):
                        fw.dma(qx[:, hp, 0:n2], QT[2][hp, :, i02:i02 + n2], writes=[rqx])

                def part1(s, m):
                    i0, n, tau_min, nblk, nks = geom(s)
                    W = 512 * nks
                    a, nq = TILE_GEOM[m]
                    qt = s * 3 + m
                    iq = i0 + a
                    qix, rqix = qixb[s % 2]
                    if m == 0 and s + 1 < 3:
                        load_qix(s + 1)
                    acc, racc = accs2[qt % 2]
                    selq, rselq = selqb[qt % 2]
                    dm, rdm = dm16b[qt % 2]
                    fw.dma(dm[0:nq, 0:W], dmask[iq:iq + nq, 0:W], writes=[rdm], q="pool")
                    dg, rdg = dgb[qt % 2]
                    seq = [(ks, hh) for ks in range(nks) for hh in range(16)]
                    rr = {}

                    def iA(i):
                        ks, hh = seq[i]
                        hp, half = hh // 2, hh % 2
                        p, rp = next_ps()
                        mm(p[0:nq, 0:512], qix[64 * half:64 * half + 64, hp, a:a + nq],
                           kix[64 * half:64 * half + 64, 512 * ks:512 * ks + 512], True, True, [rqix, rkix], [rp])
                        r_, rr_ = rb_[i % 4]
                        actf(r_[0:nq, :], p[0:nq, 0:512], AF.Relu, [rp, rwabs], [rr_], scale=wabs[0:nq, qt, hh:hh + 1])
                        rr[i] = (r_, rr_)

                    def iB(i):
                        ks, hh = seq[i]
                        r_, rr_ = rr.pop(i)
                        psc, rpsc = pO[ks % 2]
                        mm(psc[0:nq, 0:512], dg[0:nq, hh, 0:nq], r_[0:nq, :], hh == 0, False, [rdg, rr_], [rpsc])
                        if hh == 15:
                            mm(psc[0:nq, 0:512], ident[0:nq, 0:nq], dm[0:nq, 512 * ks:512 * ks + 512], False, True, [rcm, rdm], [rpsc])
                            actf(acc[0:nq, 512 * ks:512 * ks + 512], psc[0:nq, 0:512], AF.Copy, [rpsc], [racc])

                    ns = len(seq)
                    iA(0)
                    iA(1)
                    for i in range(ns):
                        if i + 2 < ns:
                            iA(i + 2)
                        iB(i)
                        if qt + 1 < 9 and i % 6 == 0 and i // 6 < 16:
                            build_dg(qt + 1, [i // 6])
                        yield
                    fw.op(dve, lambda e: e.tensor_reduce(sm[0:nq, 0:1], acc[0:nq, 0:W], axis=AX.X, op=ALU.max), [racc], [rsm])
                    fw.op(dve, lambda e: e.scalar_tensor_tensor(selq[0:nq, 0:W], acc[0:nq, 0:W], -1e29, acc[0:nq, 0:W],
                                                                op0=ALU.is_gt, op1=ALU.mult), [racc], [rselq])
                    fw.op(dve, lambda e: e.tensor_reduce(sm[0:nq, 1:2], selq[0:nq, 0:W], axis=AX.X, op=ALU.min), [rselq], [rsm])
                    fw.op(dve, lambda e: e.tensor_scalar(sm[0:nq, 2:3], sm[0:nq, 1:2], 0.0, 1.02, op0=ALU.min, op1=ALU.mult), [rsm], [rsm])
                    fw.op(dve, lambda e: e.tensor_scalar(sm[0:nq, 2:3], sm[0:nq, 2:3], -1.0, None, op0=ALU.add), [rsm], [rsm])
                    fw.op(dve, lambda e: e.scalar_tensor_tensor(sm[0:nq, 3:4], sm[0:nq, 0:1], 1.0, sm[0:nq, 2:3],
                                                                op0=ALU.add, op1=ALU.subtract), [rsm], [rsm])
                    fw.op(dve, lambda e: e.tensor_scalar(wall[0:nq, :], ckt[0:nq, :], sm[0:nq, 3:4], None, op0=ALU.mult), [rsm, rckt], [rwall])
                    fw.op(dve, lambda e: e.tensor_tensor(sm[0:nq, 4:5], sm[0:nq, 2:3], wall[0:nq, 0:1], op=ALU.add), [rsm, rwall], [rsm])
                    for it in range(NIT):
                        fw.op(dve, lambda e: e.tensor_scalar(selq[0:nq, 0:W], acc[0:nq, 0:W], sm[0:nq, 4:5], 0.0,
                                                             op0=ALU.is_gt, op1=ALU.add, accum_out=sm[0:nq, 5:6]),
                              [racc, rsm], [rselq, rsm])
                        off = -0.5 if it < NIT - 1 else -1.0
                        fw.op(dve, lambda e: e.tensor_scalar(sm[0:nq, 6:7], sm[0:nq, 5:6], 256.0, off,
                                                             op0=ALU.is_ge, op1=ALU.add), [rsm], [rsm])
                        fw.op(dve, lambda e: e.scalar_tensor_tensor(sm[0:nq, 4:5], sm[0:nq, 6:7], wall[0:nq, it:it + 1], sm[0:nq, 4:5],
                                                                    op0=ALU.mult, op1=ALU.add), [rsm, rwall], [rsm])
                    fw.op(dve, lambda e: e.tensor_scalar(selq[0:nq, 0:W], acc[0:nq, 0:W], sm[0:nq, 4:5], None, op0=ALU.is_gt),
                          [racc, rsm], [rselq])

                def part2(s, m):
                    i0, n, tau_min, nblk, nks = geom(s)
                    a, nq = TILE_GEOM[m]
                    selq, rselq = selqb[(s * 3 + m) % 2]
                    for b0 in range(0, nblk, 4):
                        nb_ = min(4, nblk - b0)
                        for j in range(nb_):
                            b = b0 + j
                            fw.op(pe, lambda e: e.transpose(pT[:, 128 * j:128 * j + nq], selq[0:nq, 128 * b:128 * b + 128], ident[0:nq, 0:nq]),
                                  [rselq, rcm], [rpT])
                        src = pT[:, 0:128 * nb_].rearrange("p (j q) -> p j q", q=128)[:, :, 0:nq]
                        actf(selT[:, b0:b0 + nb_, a:a + nq], src, AF.Identity, [rpT, rcst], [rselT], scale=30000.0, bias=cst[:, 2:3])
                        yield
                    if m == 2:
                        fw.dma(SELS[s][:, 0:nblk, 0:n], selT[:, 0:nblk, 0:n], reads=[rselT], writes=[rSEL[s]], q="act")

                load_qix(0)
                build_dg(0)
                yield from part1(*tiles[0])
                for i, (s_, m_) in enumerate(tiles):
                    if i + 1 < len(tiles):
                        yield from part1(*tiles[i + 1])
                    else:
                        yield "HOLD"
                    yield from part2(s_, m_)

            selg = sel_gen()
            sel_steps = 0
            for s_ in range(3):
                _, _, _, nblk_, nks_ = geom(s_)
                sel_steps += 3 * (nks_ * 16 + (nblk_ + 3) // 4)
            sb_blocks = 8 * sum((OWN0 + i0 + n - 2) // 128 + 1 for (i0, n) in QS)
            pump_state = [0, 0]

            def pump():
                pump_state[1] += 1
                target = min(sel_steps, (sel_steps * pump_state[1] * 100) // (sb_blocks * 80) + 2)
                while pump_state[0] < target:
                    try:
                        if next(selg) == "HOLD":
                            pump_state[0] = 1 << 30
                            return
                    except StopIteration:
                        pump_state[0] = 1 << 30
                        return
                    pump_state[0] += 1

            mask_idx = {}
            mi = 0
            for s, (i0, n) in enumerate(QS):
                b_hi = (OWN0 + i0 + n - 2) // 128
                b_full = (OWN0 + i0 - 128) // 128
                for b in range(b_full + 1, b_hi + 1):
                    mask_idx[(s, b)] = mi
                    mi += 1
            assert mi == 11, mi
            for h in range(8):
                kt, rkt = kb[h % 2]
                vt, rvt = vb[h % 2]
                qt_, rqt = qb[h % 2]
                fw.dma(kt[:], KT[0][h], writes=[rkt])
                fw.dma(vt[:], VS[0].rearrange("(b p) f -> p b f", p=128)[:, :, 128 * h:128 * h + 128], writes=[rvt])
                fw.dma(qt_[:], QT[0][h], writes=[rqt])
                for s, (i0, n) in enumerate(QS):
                    b_hi = (OWN0 + i0 + n - 2) // 128
                    blocks = list(range(b_hi, -1, -1))
                    po, rpo = (pgen4, pgen5)[(h * 3 + s) % 2]
                    q_ap = qt_[:, i0:i0 + n]
                    st = {}

                    def stage1(b, k):
                        pz, rpz = next_ps()
                        mm(pz[:, 0:n], kt[:, 128 * b:128 * b + 128], q_ap, True, True, [rkt, rqt], [rpz])
                        e_, re_ = eb[k % 3]
                        actf(e_[:, 0:n], pz[:, 0:n], AF.Exp, [rpz], [re_])
                        sp_, rsp = spb[k % 4]
                        actf(sp_[:, 0:n], e_[:, 0:n], AF.Ln, [re_, rcst], [rsp], bias=cst[:, 1:2], scale=1.0)
                        if (s, b) in mask_idx:
                            m_ = sbm[:, mask_idx[(s, b)], 0:n]
                            spm, rspm = spmb[k % 4]
                            fw.op(pool, lambda e: e.tensor_tensor(spm[:, 0:n], sp_[:, 0:n], m_, op=ALU.mult), [rsp, rsbm], [rspm])
                            st[b] = (spm, rspm)
                        else:
                            st[b] = (sp_, rsp)

                    accs = {}
                    wms = {}

                    def stage2a(b, k):
                        spm, rspm = st[b]
                        first = (k == 0)
                        if k == 1:
                            accs[k] = st[blocks[0]]
                        elif k >= 2:
                            an, ran = accb[k % 3]
                            ap_, rap = accs[k - 1]
                            sprev, rsprev = st[blocks[k - 1]]
                            fw.op(pool, lambda e: e.tensor_tensor(an[:, 0:n], ap_[:, 0:n], sprev[:, 0:n], op=ALU.add), [rap, rsprev], [ran])
                            accs[k] = (an, ran)
                        pa, rpa = next_ps()
                        mm(pa[:, 0:n], kt[:, 128 * b:128 * b + 128], q_ap, True, False, [rkt, rqt], [rpa])
                        mm(pa[:, 0:n], Ltri, spm[:, 0:n], False, first, [rcm, rspm], [rpa])
                        if not first:
                            ap_, rap = accs[k]
                            mm(pa[:, 0:n], nones, ap_[:, 0:n], False, True, [rcm, rap], [rpa])
                        w_, rw_ = wb_[k % 3]
                        actf(w_[:, 0:n], pa[:, 0:n], AF.Exp, [rpa], [rw_])
                        if (s, b) in mask_idx:
                            m_ = sbm[:, mask_idx[(s, b)], 0:n]
                            wm, rwm = wmb[k % 3]
                            fw.op(pool, lambda e: e.tensor_tensor(wm[:, 0:n], w_[:, 0:n], m_, op=ALU.mult), [rw_, rsbm], [rwm])
                        else:
                            wm, rwm = w_, rw_
                        wms[k] = (wm, rwm)

                    def stage2b(b, k):
                        wm, rwm = wms.pop(k)
                        mm(po[:, 0:n], vt[:, b, :], wm[:, 0:n], k == 0, k == nb - 1, [rvt, rwm], [rpo])

                    nb = len(blocks)
                    stage1(blocks[0], 0)
                    pw, rpw = next_ps()
                    for _ in range(WARM_N):
                        mm(pw[:, 0:512], kt[:, 0:128], kt[:, 0:512], True, True, [rkt], [rpw])
                    if nb > 1:
                        stage1(blocks[1], 1)
                    stage2a(blocks[0], 0)
                    for k in range(nb):
                        if k + 2 < nb:
                            stage1(blocks[k + 2], k + 2)
                        if k + 1 < nb:
                            stage2a(blocks[k + 1], k + 1)
                        stage2b(blocks[k], k)
                        pump()
                    evac(o_sb[:, h, i0:i0 + n], po[:, 0:n], [rpo], [ro_sb])
            for _ in selg:
                pass
        fw.barrier()

        o_ds, ro_ds = sbt(LS, "o_ds", [128, 8, NQ], BF16)
        with scope(RS) as c3:
            ebt, rebt = sbt(c3, "ebt", [128, 8, TW], BF16)
            rbt, rrbt = sbt(c3, "rbt", [128, 8], F32)
            fw.dma(rbt, rb15, writes=[rrbt])
            for h in range(8):
                fw.dma(ebt[:, h, :], tbias[:, h, :], writes=[rebt], q="pool")
            selTb = [sbt(c3, f"selT{i}", [128, 32, 342], BF16) for i in range(2)]
            kb = [sbt(c3, f"dkb{i}", [128, 4096], BF16) for i in range(2)]
            vb = [sbt(c3, f"dvb{i}", [128, 32, 128], BF16) for i in range(2)]
            qb = [sbt(c3, f"dqb{i}", [128, 342], BF16) for i in range(2)]
            pmb = [sbt(c3, f"dpm{i}", [128, 342], BF16) for i in range(4)]
            osb = [sbt(c3, f"dos{i}", [128, 342], BF16) for i in range(2)]
            rdb = [sbt(c3, f"rden{i}", [128, 342], F32) for i in range(2)]
            o32b = [sbt(c3, f"o32{i}", [128, 342], F32) for i in range(2)]
            hcount = [0]

            def emit_head(s, h):
                i0, n, tau_min, nblk, nks = geom(s)
                selT_, rselT_ = selTb[s % 2]
                b_near = max(0, (tau_min - 255) // 128 + 1)
                hc = hcount[0]
                hcount[0] += 1
                kt, rkt = kb[hc % 2]
                vt, rvt = vb[hc % 2]
                qt_, rqt = qb[hc % 2]
                fw.dma(kt[:, 0:128 * nblk], KT[1][h, :, 0:128 * nblk], writes=[rkt])
                fw.dma(vt[:, 0:nblk, :], VS[1].rearrange("(b p) f -> p b f", p=128)[:, 0:nblk, 128 * h:128 * h + 128], writes=[rvt])
                fw.dma(qt_[:, 0:n], QT[1][h, :, i0:i0 + n], writes=[rqt])
                po, rpo = pO[hc % 2]
                pd, rpd = (pgen4, pgen5)[hc % 2]
                pls = {}
                pms = {}

                def sA(b):
                    pl, rpl = next_ps()
                    near = b >= b_near
                    mm(pl[:, 0:n], kt[:, 128 * b:128 * b + 128], qt_[:, 0:n], True, False, [rkt, rqt], [rpl])
                    mm(pl[:, 0:n], ident, selT_[:, b, 0:n], False, not near, [rcm, rselT_], [rpl])
                    if near:
                        delta = 128 * b - OWN0 - i0
                        jj = J0 - delta
                        assert 0 <= jj and jj + n <= TW, (jj, n)
                        mm(pl[:, 0:n], ident, ebt[:, h, jj:jj + n], False, True, [rcm, rebt], [rpl])
                    pls[b] = (pl, rpl)

                def sB(b):
                    pl, rpl = pls.pop(b)
                    pm, rpm = pmb[b % 4]
                    if b < b_near:
                        actf(pm[:, 0:n], pl[:, 0:n], AF.Exp, [rpl, rrbt], [rpm], bias=rbt[:, h:h + 1], scale=1.0)
                    else:
                        actf(pm[:, 0:n], pl[:, 0:n], AF.Exp, [rpl], [rpm])
                    pms[b] = (pm, rpm)

                def sC(b):
                    pm, rpm = pms.pop(b)
                    mm(po[:, 0:n], vt[:, b, :], pm[:, 0:n], b == 0, b == nblk - 1, [rvt, rpm], [rpo])
                    mm(pd[:, 0:n], ones, pm[:, 0:n], b == 0, b == nblk - 1, [rcm, rpm], [rpd])

                sA(0)
                if nblk > 1:
                    sA(1)
                sB(0)
                for b in range(nblk):
                    if b + 2 < nblk:
                        sA(b + 2)
                    if b + 1 < nblk:
                        sB(b + 1)
                    sC(b)
                rd_, rrd_ = rdb[hc % 2]
                o32, ro32 = o32b[hc % 2]
                actf(rd_[:, 0:n], pd[:, 0:n], AF.Ln, [rpd, rcst], [rrd_], bias=cst[:, 3:4], scale=1.0)
                actf(rd_[:, 0:n], rd_[:, 0:n], AF.Exp, [rrd_], [rrd_], scale=-1.0)
                actf(o32[:, 0:n], po[:, 0:n], AF.Copy, [rpo], [ro32])
                fw.op(pool, lambda e: e.tensor_tensor(o_ds[:, h, i0:i0 + n], o32[:, 0:n], rd_[:, 0:n], op=ALU.mult), [ro32, rrd_], [ro_ds])

            for s in range(3):
                _, n_, _, nblk_, _ = geom(s)
                st_, rst_ = selTb[s % 2]
                fw.dma(st_[:, 0:nblk_, 0:n_], SELS[s][:, 0:nblk_, 0:n_], reads=[rSEL[s]], writes=[rst_])
                for h in range(8):
                    emit_head(s, h)
        fw.barrier()
        print("phase3 min SBUF gap bytes/partition:", gapmin[0])

        h3, rh3 = sbt(RS, "h3", [128, KC, NQ], BF16)
        mark_h3 = RS.top
        mT, rmT = sbt(RS, "mT", [128, KC, NQ], BF16)
        with scope(RS) as c4a:
            hO, rhO = sbt(c4a, "hO", [128, KC, NQ], BF16)
            with scope(RS) as c4n:
                xs, rxs = sbt(c4n, "xs4", [128, KC, 342], F32)
                csq = mk_sq(c4n, 342)
                for (i0, n) in QS:
                    fw.dma(xs[:, :, 0:n], xw[:, :, OWN0 + i0: OWN0 + i0 + n], writes=[rxs])
                    rms_core(csq, xs, rxs, n, 0, lambda kc: hO[:, kc, i0:i0 + n], [rhO])
            fw.barrier()
            wg = [sbt(c4a, f"wg{i}", [128, KC, 256], BF16) for i in range(3)]
            wp = [sbt(c4a, f"wp{i}", [128, 8, 256], BF16) for i in range(3)]
            bg, rbg = sbt(c4a, "bg", [128, 32], F32)
            fw.dma(bg[:], bgate, writes=[rbg])
            gsb = [sbt(c4a, f"gs{i}", [128, 342], F32) for i in range(4)]
            tsb = [sbt(c4a, f"ts{i}", [128, 342], F32) for i in range(4)]
            for nch in range(16):
                wgt, rwg = wg[nch % 3]
                wpt, rwp = wp[nch % 3]
                load_w(wgt[:, :, 0:128], rwg, w_in[:, :, 7248 + 128 * nch: 7248 + 128 * nch + 128], KC)
                load_w(wgt[:, :, 128:256], rwg, w_in[:, :, 9296 + 128 * nch: 9296 + 128 * nch + 128], KC)
                load_w(wpt[:, :, 0:128], rwp, w_psb[:, :, 128 * nch:128 * nch + 128], 8)
                load_w(wpt[:, :, 128:256], rwp, w_pds[:, :, 128 * nch:128 * nch + 128], 8)
                for si, (i0, n) in enumerate(QS):
                    ts_ = []
                    for br in range(2):
                        pgt, rpg = next_ps()
                        for kc in range(KC):
                            mm(pgt[:, 0:n], wgt[:, kc, 128 * br:128 * br + 128], hO[:, kc, i0:i0 + n], kc == 0, kc == KC - 1, [rwg, rhO], [rpg])
                        g_, rg_ = gsb[(si * 2 + br) % 4]
                        actf(g_[:, 0:n], pgt[:, 0:n], AF.Sigmoid, [rpg, rbg], [rg_], bias=bg[:, 16 * br + nch: 16 * br + nch + 1], scale=1.0)
                        pp, rpp = next_ps()
                        osrc, rosrc = (o_sb, ro_sb) if br == 0 else (o_ds, ro_ds)
                        for kc in range(8):
                            mm(pp[:, 0:n], wpt[:, kc, 128 * br:128 * br + 128], osrc[:, kc, i0:i0 + n], kc == 0, kc == 7, [rwp, rosrc], [rpp])
                        t_, rt_ = tsb[(si * 2 + br) % 4]
                        fw.op(dve, lambda e: e.tensor_tensor(t_[:, 0:n], pp[:, 0:n], g_[:, 0:n], op=ALU.mult), [rpp, rg_], [rt_])
                        ts_.append((t_, rt_))
                    fw.op(dve, lambda e: e.tensor_tensor(mT[:, nch, i0:i0 + n], ts_[0][0][:, 0:n], ts_[1][0][:, 0:n], op=ALU.add),
                          [ts_[0][1], ts_[1][1]], [rmT])
        fw.barrier()
        LS.top = L_consts
        x1, rx1 = sbt(LS, "x1", [128, KC, NQ], F32)
        with scope(RS) as c4b:
            wo = [sbt(c4b, f"wo{i}", [128, KC, 128], BF16) for i in range(4)]
            xi = [sbt(c4b, f"xi{i}", [128, NQ], F32) for i in range(2)]
            for nch in range(16):
                wot, rwo = wo[nch % 4]
                load_w(wot, rwo, w_out[:, :, 128 * nch:128 * nch + 128], KC)
                xt, rxt = xi[nch % 2]
                fw.dma(xt[:, 0:NQ], xw[:, nch, OWN0:OWN0 + NQ], writes=[rxt])
                for (i0, n) in QS:
                    p, rp = next_ps()
                    for kc in range(KC):
                        mm(p[:, 0:n], wot[:, kc, :], mT[:, kc, i0:i0 + n], kc == 0, kc == KC - 1, [rwo, rmT], [rp])
                    fw.op(dve, lambda e: e.tensor_tensor(x1[:, nch, i0:i0 + n], p[:, 0:n], xt[:, i0:i0 + n], op=ALU.add), [rp, rxt], [rx1])
        fw.barrier()

        hq, rhq = mT, rmT
        with scope(RS) as c5:
            hm, rhm = sbt(c5, "hm", [128, KC, 256], BF16)
            qc, rqc = sbt(c5, "qc", [128, 4, NQ], BF16)
            km, rkm = sbt(c5, "km", [128, 4, 256], BF16)
            vm, rvm = sbt(c5, "vm", [128, 2, 512], BF16)
            oc, roc = sbt(c5, "oc", [128, 4, NQ], BF16)
            wt, rwt = sbt(c5, "w5", [128, KC, 512], BF16)
            load_w(wt, rwt, w_cq, KC)
            with scope(RS) as c5n:
                csq = mk_sq(c5n, 342)
                for (i0, n) in QS:
                    rms_core(csq, x1[:, :, i0:i0 + n], rx1, n, 1, lambda kc: hq[:, kc, i0:i0 + n], [rhq])
                xm, rxm = sbt(c5n, "xm", [128, KC, 256], F32)
                fw.dma(xm, memT, writes=[rxm])
                rms_core(csq, xm, rxm, 256, 2, lambda kc: hm[:, kc, :], [rhm])
            fw.barrier()
            wt2, rwt2 = sbt(c5, "w5b", [128, KC, 512], BF16)
            load_w(wt2, rwt2, w_ckv[:, :, 0:512], KC)
            for h in range(4):
                for (i0, n) in QS:
                    p, rp = next_ps()
                    for kc in range(KC):
                        mm(p[:, 0:n], wt[:, kc, 128 * h:128 * h + 128], hq[:, kc, i0:i0 + n], kc == 0, kc == KC - 1, [rwt, rhq], [rp])
                    evac(qc[:, h, i0:i0 + n], p[:, 0:n], [rp], [rqc], scale=128 ** -0.5)
            load_w(wt, rwt, w_ckv[:, :, 512:1024], KC)
            for h in range(4):
                p, rp = next_ps()
                for kc in range(KC):
                    mm(p[:, 0:256], wt2[:, kc, 128 * h:128 * h + 128], hm[:, kc, :], kc == 0, kc == KC - 1, [rwt2, rhm], [rp])
                evac(km[:, h, :], p[:, 0:256], [rp], [rkm])
            for mt in range(2):
                p, rp = next_ps()
                for kc in range(KC):
                    mm(p[:, 0:512], hm[:, kc, 128 * mt:128 * mt + 128], wt[:, kc, :], kc == 0, kc == KC - 1, [rwt, rhm], [rp])
                evac(vm[:, mt, :], p[:, 0:512], [rp], [rvm])
            pcb = [sbt(c5, f"pc{i}", [128, 342], BF16) for i in range(3)]
            rdc, rrdc = sbt(c5, "rdc", [128, 342], F32)
            kk = 0
            for h in range(4):
                for (i0, n) in QS:
                    po, rpo = pO[0]
                    pd, rpd = pO[1]
                    for mt in range(2):
                        pl, rpl = next_ps()
                        mm(pl[:, 0:n], km[:, h, 128 * mt:128 * mt + 128], qc[:, h, i0:i0 + n], True, True, [rkm, rqc], [rpl])
                        pc_, rpc = pcb[kk % 3]
                        kk += 1
                        actf(pc_[:, 0:n], pl[:, 0:n], AF.Exp, [rpl], [rpc])
                        mm(po[:, 0:n], vm[:, mt, 128 * h:128 * h + 128], pc_[:, 0:n], mt == 0, mt == 1, [rvm, rpc], [rpo])
                        mm(pd[:, 0:n], ones, pc_[:, 0:n], mt == 0, mt == 1, [rcm, rpc], [rpd])
                    fw.op(dve, lambda e: e.reciprocal(rdc[:, 0:n], pd[:, 0:n]), [rpd], [rrdc])
                    fw.op(dve, lambda e: e.tensor_tensor(oc[:, h, i0:i0 + n], po[:, 0:n], rdc[:, 0:n], op=ALU.mult), [rpo, rrdc], [roc])
            wc = [sbt(c5, f"wc{i}", [128, 4, 128], BF16) for i in range(2)]
            for nch in range(16):
                wct, rwc = wc[nch % 2]
                load_w(wct, rwc, w_co[:, :, 128 * nch:128 * nch + 128], 4)
                for (i0, n) in QS:
                    p, rp = next_ps()
                    for kc in range(4):
                        mm(p[:, 0:n], wct[:, kc, :], oc[:, kc, i0:i0 + n], kc == 0, kc == 3, [rwc, roc], [rp])
                    fw.op(dve, lambda e: e.tensor_tensor(x1[:, nch, i0:i0 + n], p[:, 0:n], x1[:, nch, i0:i0 + n], op=ALU.add), [rp, rx1], [rx1])
        fw.barrier()
        with scope(RS) as c6n:
            csq = mk_sq(c6n, 342)
            for (i0, n) in QS:
                rms_core(csq, x1[:, :, i0:i0 + n], rx1, n, 3, lambda kc: h3[:, kc, i0:i0 + n], [rh3])
        rx2s = Res()
        for kc in range(0, KC, 4):
            fw.dma(X2S[:, kc:kc + 4, :], x1[:, kc:kc + 4, :], reads=[rx1], writes=[rx2s])
        fw.barrier()
        RS.top = mark_h3
        LS.top = L_consts

        gat, rgat = sbt(LS, "gat", [128, 48, 1024], BF16)
        with scope(RS) as c6a:
            cw, rcw = sbt(c6a, "cw", [128, 3, 96], F32)
            cb, rcb = sbt(c6a, "cb", [128, 96], F32)
            fw.dma(cw, convw, writes=[rcw])
            fw.dma(cb, convb, writes=[rcb])
            wu = [sbt(c6a, f"wu{i}", [128, KC, 128], BF16) for i in range(6)]
            ub = [sbt(c6a, f"u{i}", [128, NQ], F32) for i in range(2)]
            cbuf = [sbt(c6a, f"c{i}", [128, 1024], F32) for i in range(2)]
            ga, rga = sbt(c6a, "ga", [128, 1024], F32)
            wk = 0
            for i in range(48):
                cs = []
                for part in range(2):
                    ch = i + 48 * part
                    wt, rwt = wu[wk % 6]
                    wk += 1
                    load_w(wt, rwt, w_up[:, :, 128 * ch:128 * ch + 128], KC)
                    u_, ru = ub[part]
                    for (i0, n) in QS:
                        p, rp = next_ps()
                        for kc in range(KC):
                            mm(p[:, 0:n], wt[:, kc, :], h3[:, kc, i0:i0 + n], kc == 0, kc == KC - 1, [rwt, rh3], [rp])
                        actf(u_[:, i0:i0 + n], p[:, 0:n], AF.Copy, [rp], [ru])
                    fw.op(dve, lambda e: e.tensor_scalar(u_[:, 0:2], u_[:, 0:2], hvt[:, 0:1], None, op0=ALU.mult), [ru, rhv], [ru])
                    c_, rc_ = cbuf[part]
                    fw.op(dve, lambda e: e.tensor_scalar(c_[:, 0:1024], u_[:, 2:1026], cw[:, 2, ch:ch + 1], cb[:, ch:ch + 1],
                                                         op0=ALU.mult, op1=ALU.add), [ru, rcw, rcb], [rc_])
                    fw.op(dve, lambda e: e.scalar_tensor_tensor(c_[:, 0:1024], u_[:, 1:1025], cw[:, 1, ch:ch + 1], c_[:, 0:1024],
                                                                op0=ALU.mult, op1=ALU.add), [ru, rcw, rc_], [rc_])
                    fw.op(dve, lambda e: e.scalar_tensor_tensor(c_[:, 0:1024], u_[:, 0:1024], cw[:, 0, ch:ch + 1], c_[:, 0:1024],
                                                                op0=ALU.mult, op1=ALU.add), [ru, rcw, rc_], [rc_])
                    cs.append((c_, rc_))
                actf(ga[:, 0:1024], cs[0][0][:, 0:1024], AF.Gelu_apprx_tanh, [cs[0][1]], [rga])
                fw.op(dve, lambda e: e.tensor_tensor(gat[:, i, :], ga[:, 0:1024], cs[1][0][:, 0:1024], op=ALU.mult), [rga, cs[1][1]], [rgat])
        fw.barrier()
        RS.top = ARENA * 4
        with scope(RS) as c7:
            wd = [sbt(c7, f"wd{i}", [128, 48, 128], BF16) for i in range(3)]
            xi = [sbt(c7, f"x2i{i}", [128, 1024], F32) for i in range(2)]
            x3c = [sbt(c7, f"x3c{i}", [128, 1024], F32) for i in range(3)]
            sqc = [sbt(c7, f"sqc{i}", [128, 512], BF16) for i in range(3)]
            lnf, rlnf = sbt(c7, "lnf", [128, 1024], F32)
            rsf, rrsf = sbt(c7, "rsf", [128, 1024], F32)
            ost = [sbt(c7, f"ost{i}", [128, 1024], F32) for i in range(2)]
            pss = [pgen4, pgen5]
            rx3s = [Res() for _ in range(16)]
            sqi = 0
            for nch in range(16):
                wdt, rwd = wd[nch % 3]
                for a in range(0, 48, 12):
                    fw.dma(wdt[:, a:a + 12, :], w_down[:, a:a + 12, 128 * nch:128 * nch + 128], writes=[rwd], q="pool")
                xt, rxt = xi[nch % 2]
                fw.dma(xt, X2S[:, nch, 2:1026], reads=[rx2s], writes=[rxt])
                x3_, rx3_ = x3c[nch % 3]
                for sl in range(2):
                    a0 = 512 * sl
                    p, rp = next_ps()
                    for kc in range(48):
                        mm(p[:, 0:512], wdt[:, kc, :], gat[:, kc, a0:a0 + 512], kc == 0, kc == 47, [rwd, rgat], [rp])
                    fw.op(dve, lambda e: e.tensor_tensor(x3_[:, a0:a0 + 512], p[:, 0:512], xt[:, a0:a0 + 512], op=ALU.add), [rp, rxt], [rx3_])
                    sq_, rsq_ = sqc[sqi % 3]
                    sqi += 1
                    actf(sq_, x3_[:, a0:a0 + 512], AF.Square, [rx3_], [rsq_])
                    ps_, rps_ = pss[sl]
                    mm(ps_[:, 0:512], ones, sq_, nch == 0, nch == 15, [rsq_, rcm], [rps_])
                fw.dma(X3S[:, nch, :], x3_, reads=[rx3_], writes=[rx3s[nch]], q="act")
            for sl in range(2):
                ps_, rps_ = pss[sl]
                actf(lnf[:, 512 * sl:512 * sl + 512], ps_[:, 0:512], AF.Ln, [rps_, rcst], [rlnf], bias=cst[:, 0:1], scale=1.0 / D)
            actf(rsf, lnf, AF.Exp, [rlnf], [rrsf], scale=-0.5)
            for nch in range(16):
                x3_, rx3_ = x3c[nch % 3]
                fw.dma(x3_, X3S[:, nch, :], reads=[rx3s[nch]], writes=[rx3_])
                o_, ro_ = ost[nch % 2]
                fw.op(dve, lambda e: e.scalar_tensor_tensor(o_, x3_, gv[:, 4, nch:nch + 1], rsf, op0=ALU.mult, op1=ALU.mult),
                      [rx3_, rrsf, rgv], [ro_])
                fw.dma(outT[:, nch, :], o_, reads=[ro_], q="pool")
        fw.finish()
        print("instructions emitted:", fw.n_inst)
    return nc


def _bucket(rel):
    rel = np.asarray(rel, np.int64)
    nb = 16
    max_exact = 8
    ret = np.where(rel > 0, nb, 0)
    n = np.abs(rel)
    nf = np.maximum(n, 1).astype(np.float32)
    lg = (np.log(nf / np.float32(max_exact)).astype(np.float32) / np.float32(np.log(128 / 8))).astype(np.float32)
    large = max_exact + (lg * np.float32(nb - max_exact)).astype(np.int32)
    large = np.minimum(large, nb - 1)
    return ret + np.where(n < max_exact, n, large)


def _fm(v, n):
    return np.ascontiguousarray(np.asarray(v, np.float32).reshape(n, 128).T)


_NC_CACHE = {}


def kernel(x, mem, g_mix, w_in, b_gate, w_proj_sb, w_proj_dsa, w_out, rel_bias, g_cross, g_mem,
           w_cq, w_ckv, w_co, g_ffn, w_up, conv_w, conv_b, w_down, g_final):
    f32 = np.float32
    x = np.asarray(x, f32)
    mem = np.asarray(mem, f32)
    rel_bias = np.asarray(rel_bias, f32)
    if "nc" not in _NC_CACHE:
        _NC_CACHE["nc"] = build()
    nc = _NC_CACHE["nc"]

    shared = {
        "w_in": np.ascontiguousarray(np.asarray(w_in, f32)[0]),
        "w_proj_sb": np.ascontiguousarray(np.asarray(w_proj_sb, f32)[0]),
        "w_proj_dsa": np.ascontiguousarray(np.asarray(w_proj_dsa, f32)[0]),
        "w_out": np.ascontiguousarray(np.asarray(w_out, f32)[0]),
        "w_cq": np.ascontiguousarray(np.asarray(w_cq, f32)[0]),
        "w_ckv": np.ascontiguousarray(np.asarray(w_ckv, f32)[0]),
        "w_co": np.ascontiguousarray(np.asarray(w_co, f32)[0]),
        "w_up": np.ascontiguousarray(np.asarray(w_up, f32)[0]),
        "w_down": np.ascontiguousarray(np.asarray(w_down, f32)[0]),
    }
    gs = np.stack([_fm(np.asarray(g, f32).reshape(-1), 16) for g in (g_mix, g_cross, g_mem, g_ffn, g_final)], axis=1)
    shared["gvec"] = np.ascontiguousarray(gs)
    shared["bgate"] = _fm(np.asarray(b_gate, f32).reshape(-1), 32)
    cwv = np.asarray(conv_w, f32)[0]
    shared["convw"] = np.ascontiguousarray(np.stack([_fm(cwv[i], 96) for i in range(3)], axis=1))
    shared["convb"] = _fm(np.asarray(conv_b, f32).reshape(-1), 96)
    p = np.arange(128)[:, None]
    jj = np.arange(TW)[None, :]
    bk = _bucket(p - (jj - J0))
    shared["tbias"] = np.ascontiguousarray(np.transpose(rel_bias[bk], (0, 2, 1)))
    shared["rb15"] = np.ascontiguousarray(np.broadcast_to(rel_bias[15][None, :], (128, 8)))
    sbm = np.zeros((128, 11, 342), f32)
    mi = 0
    for s, (i0, n) in enumerate(QS):
        b_hi = (OWN0 + i0 + n - 2) // 128
        b_full = (OWN0 + i0 - 128) // 128
        for b in range(b_full + 1, b_hi + 1):
            ks = 128 * b + np.arange(128)[:, None]
            qs = OWN0 + i0 + np.arange(342)[None, :]
            sbm[:, mi, :] = (ks < qs).astype(f32)
            mi += 1
    shared["sbm"] = sbm
    cm = np.zeros((128, 4, 128), f32)
    jx = np.arange(128)[:, None]
    sx = np.arange(128)[None, :]
    cm[:, 0, :] = -(jx >= sx).astype(f32)
    cm[:, 1, :] = -1.0
    cm[:, 2, :] = 1.0
    cm[:, 3, :] = np.eye(128, dtype=f32)
    shared["cmat"] = cm
    shared["ckv"] = np.ascontiguousarray(np.broadcast_to((2.0 ** -(np.arange(NIT) + 1.0)).astype(f32)[None, :], (128, NIT)))

    in_maps = []
    for c in range(8):
        b, j = c // 4, c % 4
        t0 = j * 1024
        win = np.zeros((4096, D), f32)
        lo = t0 - 3072
        src0 = max(lo, 0)
        win[src0 - lo:, :] = x[b, src0:t0 + 1024, :]
        xwin = np.ascontiguousarray(win.T.reshape(16, 128, 4096).transpose(1, 0, 2))
        mT = np.ascontiguousarray(mem[b].T.reshape(16, 128, 256).transpose(1, 0, 2))
        tq = t0 - 2 + np.arange(NQ)[:, None]
        tk = lo + np.arange(4096)[None, :]
        vis = (tk >= 0) & ((tk // 64) <= (tq // 64))
        dm = np.where(vis, 0.0, -1e30).astype(f32)
        m = dict(shared)
        m["xw"] = xwin
        m["memT"] = mT
        m["dmask"] = np.ascontiguousarray(dm)
        m["hv"] = np.full((128, 1), 1.0 if j > 0 else 0.0, f32)
        in_maps.append(m)

    res = run_bass_kernel_spmd(nc, in_maps, core_ids=list(range(8)))
    out = np.zeros((2, 4096, D), f32)
    for c in range(8):
        b, j = c // 4, c % 4
        o = np.asarray(res.results[c]["outT"], f32)
        out[b, j * 1024:(j + 1) * 1024, :] = o.transpose(2, 1, 0).reshape(1024, D)
    if DEBUG:
        kernel.last = res
    return out
```

```python
import numpy as np
from contextlib import ExitStack, contextmanager
import concourse.bass as bass
import concourse.mybir as mybir
from concourse.bass_utils import run_bass_kernel_spmd

F32 = mybir.dt.float32
BF16 = mybir.dt.bfloat16
AF = mybir.ActivationFunctionType
ALU = mybir.AluOpType
AX = mybir.AxisListType

D = 2048
KC = 16
NQ = 1026
OWN0 = 3070
QS = [(0, 342), (342, 342), (684, 342)]
EPS = 1e-6
NIT = 24
TW = 1000
J0 = 404
DFF = 6144
DEBUG = False
WARM_N = 16


class Res:
    __slots__ = ("w", "r")

    def __init__(self):
        self.w = None
        self.r = []


class Eng:
    def __init__(self, name, obj, sem, is_pe=False):
        self.name = name
        self.obj = obj
        self.sem = sem
        self.count = 0
        self.waited = {}
        self.is_pe = is_pe


class FW:
    def __init__(self, nc, ctx, n_dsem=12):
        self.nc = nc
        mk = lambda n: ctx.enter_context(nc.semaphore(n))
        self.pe = Eng("pe", nc.tensor, mk("s_pe"), is_pe=True)
        self.act = Eng("act", nc.scalar, mk("s_act"))
        self.dve = Eng("dve", nc.vector, mk("s_dve"))
        self.pool = Eng("pool", nc.gpsimd, mk("s_pool"))
        self.sp = Eng("sp", nc.sync, mk("s_sp"))
        self.engs = [self.pe, self.act, self.dve, self.pool, self.sp]
        self.dsems = {}
        for q in ("sp", "pool", "act"):
            self.dsems[q] = [[mk(f"d_{q}{i}"), 0] for i in range(n_dsem if q != "act" else 4)]
        self.dma_i = {"sp": 0, "pool": 0, "act": 0}
        self.n_inst = 0

    def _wait(self, eng, tok):
        if tok is None:
            return
        sem, val, owner = tok
        if owner is eng and eng.is_pe:
            return
        key = id(sem)
        if eng.waited.get(key, 0) >= val:
            return
        eng.obj.wait_ge(sem, val)
        eng.waited[key] = val

    def _deps(self, eng, reads, writes):
        for r in reads:
            self._wait(eng, r.w)
        for w in writes:
            self._wait(eng, w.w)
            for t in w.r:
                self._wait(eng, t)

    def _record(self, tok, reads, writes):
        for r in reads:
            r.r.append(tok)
            if len(r.r) > 24:
                best = {}
                for t in r.r:
                    k = id(t[0])
                    if k not in best or best[k][1] < t[1]:
                        best[k] = t
                r.r = list(best.values())
        for w in writes:
            w.w = tok
            w.r = []

    def op(self, eng, fn, reads=(), writes=()):
        self._deps(eng, reads, writes)
        inst = fn(eng.obj)
        eng.count += 1
        inst.then_inc(eng.sem, 1)
        tok = (eng.sem, eng.count, eng)
        self._record(tok, reads, writes)
        self.n_inst += 1
        return tok

    def dma(self, out, in_, reads=(), writes=(), q="sp"):
        eng = {"sp": self.sp, "pool": self.pool, "act": self.act}[q]
        slots = self.dsems[q]
        i = self.dma_i[q]
        self.dma_i[q] = i + 1
        slot = slots[i % len(slots)]
        if slot[1] > 0:
            self._wait(eng, (slot[0], slot[1], None))
        self._deps(eng, reads, writes)
        inst = eng.obj.dma_start(out=out, in_=in_)
        slot[1] += 16
        inst.then_inc(slot[0], 16)
        tok = (slot[0], slot[1], None)
        self._record(tok, reads, writes)
        self.n_inst += 1
        return tok

    def all_tokens(self):
        toks = []
        for e in self.engs:
            if e.count > 0:
                toks.append((e.sem, e.count, e))
        for q in self.dsems:
            for s in self.dsems[q]:
                if s[1] > 0:
                    toks.append((s[0], s[1], None))
        return toks

    def barrier(self):
        toks = self.all_tokens()
        for e in self.engs:
            for t in toks:
                if t[2] is e:
                    continue
                self._wait(e, t)

    def finish(self):
        for t in self.all_tokens():
            self._wait(self.sp, t)


def build():
    nc = bass.Bass("TRN2", target_bir_lowering=False)

    def din(name, shape, dtype=F32):
        return nc.dram_tensor(name, shape, dtype, kind="ExternalInput").ap()

    xw = din("xw", [128, KC, 4096])
    memT = din("memT", [128, KC, 256])
    w_in = din("w_in", [D, 11344]).rearrange("(kc p) n -> p kc n", p=128)
    w_psb = din("w_proj_sb", [1024, D]).rearrange("(kc p) n -> p kc n", p=128)
    w_pds = din("w_proj_dsa", [1024, D]).rearrange("(kc p) n -> p kc n", p=128)
    w_out = din("w_out", [D, D]).rearrange("(kc p) n -> p kc n", p=128)
    w_cq = din("w_cq", [D, 512]).rearrange("(kc p) n -> p kc n", p=128)
    w_ckv = din("w_ckv", [D, 1024]).rearrange("(kc p) n -> p kc n", p=128)
    w_co = din("w_co", [512, D]).rearrange("(kc p) n -> p kc n", p=128)
    w_up = din("w_up", [D, 2 * DFF]).rearrange("(kc p) n -> p kc n", p=128)
    w_down = din("w_down", [DFF, D]).rearrange("(kc p) n -> p kc n", p=128)
    gvec = din("gvec", [128, 5, KC])
    bgate = din("bgate", [128, 32])
    convw = din("convw", [128, 3, 96])
    convb = din("convb", [128, 96])
    dmask = din("dmask", [NQ, 4096])
    tbias = din("tbias", [128, 8, TW])
    rb15 = din("rb15", [128, 8])
    sbm_d = din("sbm", [128, 11, 342])
    cmat = din("cmat", [128, 4, 128])
    hv_d = din("hv", [128, 1])
    ckv_d = din("ckv", [128, NIT])
    outT = nc.dram_tensor("outT", [128, KC, 1024], F32, kind="ExternalOutput").ap()

    skind = "ExternalOutput" if DEBUG else "Internal"
    KT = [nc.dram_tensor(f"KT{i}", [8, 128, 4096], BF16, kind=skind).ap() for i in range(2)]
    VS = [nc.dram_tensor(f"VS{i}", [4096, 1024], BF16, kind=skind).ap() for i in range(2)]
    QT = [nc.dram_tensor(f"QT{i}", [8, 128, NQ], BF16, kind=skind).ap() for i in range(3)]
    X2S = nc.dram_tensor("X2S", [128, KC, NQ], F32, kind=skind).ap()
    SELS = [nc.dram_tensor(f"SELS{i}", [128, 32, 342], BF16, kind=skind).ap() for i in range(3)]
    X3S = nc.dram_tensor("X3S", [128, KC, 1024], F32, kind=skind).ap()
    OSC = [nc.dram_tensor(f"OSC{i}", [128, 8, NQ], BF16, kind=skind).ap() for i in range(2)]
    if DEBUG:
        DBG = nc.dram_tensor("DBG", [128, 16, NQ], F32, kind="ExternalOutput").ap()

    with ExitStack() as ctx:
        fw = FW(nc, ctx)
        pe, act, dve, pool = fw.pe, fw.act, fw.dve, fw.pool

        ARENA = 52800
        big = ctx.enter_context(nc.sbuf_tensor("arena", [128, ARENA], F32))

        class Stk:
            def __init__(self, left):
                self.left = left
                self.top = 0 if left else ARENA * 4

            def alloc(self, shape, dtype):
                nel = 1
                for d_ in shape[1:]:
                    nel *= d_
                nb = nel * (2 if dtype == BF16 else 4)
                nb = (nb + 63) // 64 * 64
                if self.left:
                    off = self.top
                    self.top += nb
                else:
                    self.top -= nb
                    off = self.top
                assert LS.top <= RS.top, ("SBUF arena overflow", LS.top, RS.top)
                gapmin[0] = min(gapmin[0], RS.top - LS.top)
                v = big[:, off // 4: (off + nb) // 4]
                if dtype == BF16:
                    v = v.bitcast(BF16)
                v = v[:, 0:nel]
                if len(shape) == 3:
                    v = v.rearrange("p (a b) -> p a b", b=shape[2])
                return v

        gapmin = [1 << 30]
        LS = Stk(True)
        RS = Stk(False)

        @contextmanager
        def scope(stk):
            m_ = stk.top
            yield stk
            stk.top = m_

        def sbt(c, name, shape, dtype):
            return c.alloc(shape, dtype), Res()

        def pst(name, shape, dtype):
            return ctx.enter_context(nc.psum_tensor(name, shape, dtype)), Res()

        pgen = [pst(f"pg{i}", [128, 512], F32) for i in range(3)]
        pgen4 = pst("pg4", [128, 512], F32)
        pgen5 = pst("pg5", [128, 512], F32)
        pO = [pst(f"po{i}", [128, 512], F32) for i in range(2)]
        pS, rpS = pgen4
        pT, rpT = pst("ptr", [128, 1024], BF16)
        pgi = [0]

        def next_ps():
            p = pgen[pgi[0] % 3]
            pgi[0] += 1
            return p

        cm, rcm = sbt(LS, "cm", [128, 4, 128], BF16)
        gv, rgv = sbt(LS, "gv", [128, 5, KC], F32)
        cst, rcst = sbt(LS, "cst", [128, 4], F32)
        hvt, rhv = sbt(LS, "hvt", [128, 1], F32)
        L_consts = LS.top
        Ltri, nones, ones, ident = cm[:, 0, :], cm[:, 1, :], cm[:, 2, :], cm[:, 3, :]
        fw.dma(cm[:], cmat, writes=[rcm], q="pool")
        fw.dma(gv[:], gvec, writes=[rgv])
        fw.dma(hvt[:], hv_d, writes=[rhv])
        fw.op(dve, lambda e: e.memset(cst[:, 0:1], EPS), writes=[rcst])
        fw.op(dve, lambda e: e.memset(cst[:, 1:2], 1.0), writes=[rcst])
        fw.op(dve, lambda e: e.memset(cst[:, 2:3], -30000.0), writes=[rcst])
        fw.op(dve, lambda e: e.memset(cst[:, 3:4], 1e-30), writes=[rcst])

        def mm(out, lhsT, rhs, start, stop, reads, writes):
            return fw.op(pe, lambda e: e.matmul(out, lhsT, rhs, start=start, stop=stop), reads, writes)

        def actf(out, in_, func, reads, writes, **kw):
            return fw.op(act, lambda e: e.activation(out, in_, func, **kw), reads, writes)

        evi = [0]

        def evac(dst, src, reads, writes, scale=None):
            evi[0] += 1
            if evi[0] % 2 == 0:
                if scale is None:
                    return actf(dst, src, AF.Copy, reads, writes)
                return actf(dst, src, AF.Copy, reads, writes, scale=float(scale))
            if scale is None:
                return fw.op(dve, lambda e: e.tensor_copy(dst, src), reads, writes)
            return fw.op(dve, lambda e: e.tensor_scalar(dst, src, float(scale), None, op0=ALU.mult), reads, writes)

        def load_w(dst, rdst, src, nkc):
            h = max(1, nkc // 2)
            for a in range(0, nkc, h):
                fw.dma(dst[:, a:a + h, :], src[:, a:a + h, :], writes=[rdst], q="pool")

        def rms_core(c_sq, xs, rxs, ntok, gi, dst_fn, dst_res):
            sq, rsq, lnv, rln, rs, rrs = c_sq
            actf(sq[:, :, 0:ntok], xs[:, :, 0:ntok], AF.Square, [rxs], [rsq])
            for kc in range(KC):
                mm(pS[:, 0:ntok], ones, sq[:, kc, 0:ntok], kc == 0, kc == KC - 1, [rsq, rcm], [rpS])
            actf(lnv[:, 0:ntok], pS[:, 0:ntok], AF.Ln, [rpS, rcst], [rln], bias=cst[:, 0:1], scale=1.0 / D)
            actf(rs[:, 0:ntok], lnv[:, 0:ntok], AF.Exp, [rln], [rrs], scale=-0.5)
            for kc in range(KC):
                fw.op(dve, lambda e: e.scalar_tensor_tensor(dst_fn(kc), xs[:, kc, 0:ntok], gv[:, gi, kc:kc + 1],
                                                            rs[:, 0:ntok], op0=ALU.mult, op1=ALU.mult),
                      [rxs, rrs, rgv], dst_res)

        def mk_sq(c, w):
            sq, rsq = sbt(c, "sq", [128, KC, w], BF16)
            lnv, rln = sbt(c, "lnv", [128, w], F32)
            rs, rrs = sbt(c, "rs", [128, w], F32)
            return (sq, rsq, lnv, rln, rs, rrs)

        kix, rkix = sbt(LS, "kix", [128, 4096], BF16)
        wabs, rwabs = sbt(LS, "wabs", [128, 9, 16], F32)
        wsgn, rwsgn = sbt(LS, "wsgn", [128, 9, 16], F32)
        QT_tiles = []
        for s, (i0, n) in enumerate(QS):
            for m, (a, nq) in enumerate([(0, 128), (128, 128), (256, 86)]):
                QT_tiles.append((s, i0 + a, nq))

        with scope(RS) as c1:
            hT, _ = sbt(c1, "hT", [128, KC, 2048], BF16)
            hres = [Res() for _ in range(8)]
            xsb = [sbt(c1, f"xs{i}", [128, KC, 256], F32) for i in range(2)]
            csq = mk_sq(c1, 256)
            wts = [sbt(c1, f"wt{i}", [128, KC, 256], BF16) for i in range(4)]
            stg = [sbt(c1, f"stg{i}", [128, 512], BF16) for i in range(4)]
            wi = [0]
            si = [0]

            def nwt():
                w = wts[wi[0] % 4]
                wi[0] += 1
                return w

            def nstg():
                s_ = stg[si[0] % 4]
                si[0] += 1
                return s_

            def hr(a, b):
                return hres[a // 256:(b - 1) // 256 + 1]

            for g in range(2):
                for sl in range(8):
                    xs, rxs = xsb[sl % 2]
                    fw.dma(xs[:], xw[:, :, g * 2048 + sl * 256: g * 2048 + sl * 256 + 256], writes=[rxs])
                    rms_core(csq, xs, rxs, 256, 0, lambda kc: hT[:, kc, sl * 256: sl * 256 + 256], [hres[sl]])
                for br, (koff, voff) in enumerate([(1024, 2048), (4096, 5120)]):
                    for h in range(8):
                        wt, rwt = nwt()
                        load_w(wt[:, :, 0:128], rwt, w_in[:, :, koff + 128 * h: koff + 128 * h + 128], KC)
                        for s4 in range(4):
                            p, rp = next_ps()
                            for kc in range(KC):
                                mm(p[:, 0:512], wt[:, kc, 0:128], hT[:, kc, 512 * s4: 512 * s4 + 512], kc == 0, kc == KC - 1,
                                   [rwt] + hr(512 * s4, 512 * s4 + 512), [rp])
                            st, rst = nstg()
                            evac(st[:, 0:512], p[:, 0:512], [rp], [rst])
                            fw.dma(KT[br][h, :, g * 2048 + 512 * s4: g * 2048 + 512 * s4 + 512], st[:, 0:512], reads=[rst])
                    for cg in range(4):
                        wt, rwt = nwt()
                        load_w(wt[:, :, 0:256], rwt, w_in[:, :, voff + 256 * cg: voff + 256 * cg + 256], KC)
                        for blk in range(16):
                            p, rp = next_ps()
                            for kc in range(KC):
                                mm(p[:, 0:256], hT[:, kc, 128 * blk: 128 * blk + 128], wt[:, kc, 0:256], kc == 0, kc == KC - 1,
                                   [rwt] + hr(128 * blk, 128 * blk + 128), [rp])
                            st, rst = nstg()
                            evac(st[:, 0:256], p[:, 0:256], [rp], [rst])
                            r0 = g * 2048 + 128 * blk
                            fw.dma(VS[br][r0:r0 + 128, 256 * cg: 256 * cg + 256], st[:, 0:256], reads=[rst])
                wt, rwt = nwt()
                fw.dma(wt[:, :, 0:64], w_in[:, :, 7168:7232], writes=[rwt], q="pool")
                fw.dma(wt[:, :, 64:128], w_in[:, :, 7168:7232], writes=[rwt], q="pool")
                for s4 in range(4):
                    p, rp = next_ps()
                    for kc in range(KC):
                        mm(p[:, 0:512], wt[:, kc, 0:128], hT[:, kc, 512 * s4: 512 * s4 + 512], kc == 0, kc == KC - 1,
                           [rwt] + hr(512 * s4, 512 * s4 + 512), [rp])
                    evac(kix[:, g * 2048 + 512 * s4: g * 2048 + 512 * s4 + 512], p[:, 0:512], [rp], [rkix])
                if g == 1:
                    OB = 1022
                    for qi, (off, scale) in enumerate([(0, 128 ** -0.5), (3072, 128 ** -0.5), (6144, 0.125)]):
                        for h in range(8):
                            wt, rwt = nwt()
                            load_w(wt[:, :, 0:128], rwt, w_in[:, :, off + 128 * h: off + 128 * h + 128], KC)
                            for (i0, n) in QS:
                                p, rp = next_ps()
                                for kc in range(KC):
                                    mm(p[:, 0:n], wt[:, kc, 0:128], hT[:, kc, OB + i0: OB + i0 + n], kc == 0, kc == KC - 1,
                                       [rwt] + hr(OB + i0, OB + i0 + n), [rp])
                                st, rst = nstg()
                                evac(st[:, 0:n], p[:, 0:n], [rp], [rst], scale=scale)
                                fw.dma(QT[qi][h, :, i0:i0 + n], st[:, 0:n], reads=[rst])
                    wt, rwt = nwt()
                    fw.dma(wt[:, :, 0:16], w_in[:, :, 7232:7248], writes=[rwt], q="pool")
                    for qt, (s, iq, nq) in enumerate(QT_tiles):
                        p, rp = next_ps()
                        for kc in range(KC):
                            mm(p[0:nq, 0:16], hT[:, kc, OB + iq: OB + iq + nq], wt[:, kc, 0:16], kc == 0, kc == KC - 1,
                               [rwt] + hr(OB + iq, OB + iq + nq), [rp])
                        actf(wabs[0:nq, qt, :], p[0:nq, 0:16], AF.Abs, [rp], [rwabs], scale=0.25)
                        actf(wsgn[0:nq, qt, :], p[0:nq, 0:16], AF.Sign, [rp], [rwsgn])
        fw.barrier()

        L_kix = LS.top
        o_sb, ro_sb = sbt(LS, "o_sb", [128, 8, NQ], BF16)

        rSEL = [Res() for _ in range(3)]
        TILE_GEOM = [(0, 128), (128, 128), (256, 86)]

        def geom(s):
            i0, n = QS[s]
            tau_min = OWN0 + i0
            tau_max = OWN0 + i0 + n - 1
            vis_end = (tau_max // 64 + 1) * 64
            nblk = min((vis_end + 127) // 128, 32)
            nks = min((vis_end + 511) // 512, 8)
            return i0, n, tau_min, nblk, nks

        with scope(RS) as c2:
            sbm, rsbm = sbt(c2, "sbm", [128, 11, 342], BF16)
            fw.dma(sbm[:], sbm_d, writes=[rsbm], q="pool")
            kb = [sbt(c2, f"kb{i}", [128, 4096], BF16) for i in range(2)]
            vb = [sbt(c2, f"vb{i}", [128, 32, 128], BF16) for i in range(2)]
            qb = [sbt(c2, f"qb{i}", [128, NQ], BF16) for i in range(2)]
            eb = [sbt(c2, f"e{i}", [128, 342], F32) for i in range(3)]
            spb = [sbt(c2, f"sp{i}", [128, 342], BF16) for i in range(4)]
            spmb = [sbt(c2, f"spm{i}", [128, 342], BF16) for i in range(4)]
            wb_ = [sbt(c2, f"w{i}", [128, 342], BF16) for i in range(3)]
            wmb = [sbt(c2, f"wm{i}", [128, 342], BF16) for i in range(3)]
            accb = [sbt(c2, f"acc{i}", [128, 342], BF16) for i in range(3)]
            qixb = [sbt(c2, f"qix{i}", [128, 8, 342], BF16) for i in range(2)]
            ckt, rckt = sbt(c2, "ckt", [128, NIT], F32)
            fw.dma(ckt, ckv_d, writes=[rckt])
            accs2 = [sbt(c2, f"dacc{i}", [128, 4096], F32) for i in range(2)]
            dm16b = [sbt(c2, f"dm16{i}", [128, 4096], BF16) for i in range(2)]
            selqb = [sbt(c2, f"selq{i}", [128, 4096], BF16) for i in range(2)]
            selT, rselT = sbt(c2, "selTs", [128, 32, 342], BF16)
            rb_ = [sbt(c2, f"dr{i}", [128, 512], BF16) for i in range(4)]
            dgb = [sbt(c2, f"dg{i}", [128, 16, 128], BF16) for i in range(2)]
            sm, rsm = sbt(c2, "dsm", [128, 8], F32)
            wall, rwall = sbt(c2, "wall", [128, NIT], F32)

            def sel_gen():
                tiles = [(s_, m_) for s_ in range(3) for m_ in range(3)]

                def build_dg(qt2, hhs=range(16)):
                    nq2 = TILE_GEOM[qt2 % 3][1]
                    dg2, rdg2 = dgb[qt2 % 2]
                    for hh in hhs:
                        fw.op(pool, lambda e: e.tensor_scalar(dg2[0:nq2, hh, 0:nq2], ident[0:nq2, 0:nq2], wsgn[0:nq2, qt2, hh:hh + 1], None, op0=ALU.mult),
                              [rcm, rwsgn], [rdg2])

                def load_qix(s2):
                    i02, n2 = QS[s2]
                    qx, rqx = qixb[s2 % 2]
                    for hp in range(8):
                        fw.dma(qx[:, hp, 0:n2], QT[2][hp, :, i02:i02 + n2], writes=[rqx])

                def part1(s, m):
                    i0, n, tau_min, nblk, nks = geom(s)
                    W = 512 * nks
                    a, nq = TILE_GEOM[m]
                    qt = s * 3 + m
                    iq = i0 + a
                    qix, rqix = qixb[s % 2]
                    if m == 0 and s + 1 < 3:
                        load_qix(s + 1)
                    acc, racc = accs2[qt % 2]
                    selq, rselq = selqb[qt % 2]
                    dm, rdm = dm16b[qt % 2]
                    fw.dma(dm[0:nq, 0:W], dmask[iq:iq + nq, 0:W], writes=[rdm], q="pool")
                    dg, rdg = dgb[qt % 2]
                    seq = [(ks, hh) for ks in range(nks) for hh in range(16)]
                    rr = {}

                    def iA(i):
                        ks, hh = seq[i]
                        hp, half = hh // 2, hh % 2
                        p, rp = next_ps()
                        mm(p[0:nq, 0:512], qix[64 * half:64 * half + 64, hp, a:a + nq],
                           kix[64 * half:64 * half + 64, 512 * ks:512 * ks + 512], True, True, [rqix, rkix], [rp])
                        r_, rr_ = rb_[i % 4]
                        actf(r_[0:nq, :], p[0:nq, 0:512], AF.Relu, [rp, rwabs], [rr_], scale=wabs[0:nq, qt, hh:hh + 1])
                        rr[i] = (r_, rr_)

                    def iB(i):
                        ks, hh = seq[i]
                        r_, rr_ = rr.pop(i)
                        psc, rpsc = pO[ks % 2]
                        mm(psc[0:nq, 0:512], dg[0:nq, hh, 0:nq], r_[0:nq, :], hh == 0, False, [rdg, rr_], [rpsc])
                        if hh == 15:
                            mm(psc[0:nq, 0:512], ident[0:nq, 0:nq], dm[0:nq, 512 * ks:512 * ks + 512], False, True, [rcm, rdm], [rpsc])
                            actf(acc[0:nq, 512 * ks:512 * ks + 512], psc[0:nq, 0:512], AF.Copy, [rpsc], [racc])

                    ns = len(seq)
                    iA(0)
                    iA(1)
                    for i in range(ns):
                        if i + 2 < ns:
                            iA(i + 2)
                        iB(i)
                        if qt + 1 < 9 and i % 6 == 0 and i // 6 < 16:
                            build_dg(qt + 1, [i // 6])
                        yield
                    fw.op(dve, lambda e: e.tensor_reduce(sm[0:nq, 0:1], acc[0:nq, 0:W], axis=AX.X, op=ALU.max), [racc], [rsm])
                    fw.op(dve, lambda e: e.scalar_tensor_tensor(selq[0:nq, 0:W], acc[0:nq, 0:W], -1e29, acc[0:nq, 0:W],
                                                                op0=ALU.is_gt, op1=ALU.mult), [racc], [rselq])
                    fw.op(dve, lambda e: e.tensor_reduce(sm[0:nq, 1:2], selq[0:nq, 0:W], axis=AX.X, op=ALU.min), [rselq], [rsm])
                    fw.op(dve, lambda e: e.tensor_scalar(sm[0:nq, 2:3], sm[0:nq, 1:2], 0.0, 1.02, op0=ALU.min, op1=ALU.mult), [rsm], [rsm])
                    fw.op(dve, lambda e: e.tensor_scalar(sm[0:nq, 2:3], sm[0:nq, 2:3], -1.0, None, op0=ALU.add), [rsm], [rsm])
                    fw.op(dve, lambda e: e.scalar_tensor_tensor(sm[0:nq, 3:4], sm[0:nq, 0:1], 1.0, sm[0:nq, 2:3],
                                                                op0=ALU.add, op1=ALU.subtract), [rsm], [rsm])
                    fw.op(dve, lambda e: e.tensor_scalar(wall[0:nq, :], ckt[0:nq, :], sm[0:nq, 3:4], None, op0=ALU.mult), [rsm, rckt], [rwall])
                    fw.op(dve, lambda e: e.tensor_tensor(sm[0:nq, 4:5], sm[0:nq, 2:3], wall[0:nq, 0:1], op=ALU.add), [rsm, rwall], [rsm])
                    for it in range(NIT):
                        fw.op(dve, lambda e: e.tensor_scalar(selq[0:nq, 0:W], acc[0:nq, 0:W], sm[0:nq, 4:5], 0.0,
                                                             op0=ALU.is_gt, op1=ALU.add, accum_out=sm[0:nq, 5:6]),
                              [racc, rsm], [rselq, rsm])
                        off = -0.5 if it < NIT - 1 else -1.0
                        fw.op(dve, lambda e: e.tensor_scalar(sm[0:nq, 6:7], sm[0:nq, 5:6], 256.0, off,
                                                             op0=ALU.is_ge, op1=ALU.add), [rsm], [rsm])
                        fw.op(dve, lambda e: e.scalar_tensor_tensor(sm[0:nq, 4:5], sm[0:nq, 6:7], wall[0:nq, it:it + 1], sm[0:nq, 4:5],
                                                                    op0=ALU.mult, op1=ALU.add), [rsm, rwall], [rsm])
                    fw.op(dve, lambda e: e.tensor_scalar(selq[0:nq, 0:W], acc[0:nq, 0:W], sm[0:nq, 4:5], None, op0=ALU.is_gt),
                          [racc, rsm], [rselq])

                def part2(s, m):
                    i0, n, tau_min, nblk, nks = geom(s)
                    a, nq = TILE_GEOM[m]
                    selq, rselq = selqb[(s * 3 + m) % 2]
                    for b0 in range(0, nblk, 4):
                        nb_ = min(4, nblk - b0)
                        for j in range(nb_):
                            b = b0 + j
                            fw.op(pe, lambda e: e.transpose(pT[:, 128 * j:128 * j + nq], selq[0:nq, 128 * b:128 * b + 128], ident[0:nq, 0:nq]),
                                  [rselq, rcm], [rpT])
                        src = pT[:, 0:128 * nb_].rearrange("p (j q) -> p j q", q=128)[:, :, 0:nq]
                        actf(selT[:, b0:b0 + nb_, a:a + nq], src, AF.Identity, [rpT, rcst], [rselT], scale=30000.0, bias=cst[:, 2:3])
                        yield
                    if m == 2:
                        fw.dma(SELS[s][:, 0:nblk, 0:n], selT[:, 0:nblk, 0:n], reads=[rselT], writes=[rSEL[s]], q="act")

                load_qix(0)
                build_dg(0)
                yield from part1(*tiles[0])
                for i, (s_, m_) in enumerate(tiles):
                    if i + 1 < len(tiles):
                        yield from part1(*tiles[i + 1])
                    else:
                        yield "HOLD"
                    yield from part2(s_, m_)

            selg = sel_gen()
            sel_steps = 0
            for s_ in range(3):
                _, _, _, nblk_, nks_ = geom(s_)
                sel_steps += 3 * (nks_ * 16 + (nblk_ + 3) // 4)
            sb_blocks = 8 * sum((OWN0 + i0 + n - 2) // 128 + 1 for (i0, n) in QS)
            pump_state = [0, 0]

            def pump():
                pump_state[1] += 1
                target = min(sel_steps, (sel_steps * pump_state[1] * 100) // (sb_blocks * 90) + 2)
                while pump_state[0] < target:
                    try:
                        if next(selg) == "HOLD":
                            pump_state[0] = 1 << 30
                            return
                    except StopIteration:
                        pump_state[0] = 1 << 30
                        return
                    pump_state[0] += 1

            mask_idx = {}
            mi = 0
            for s, (i0, n) in enumerate(QS):
                b_hi = (OWN0 + i0 + n - 2) // 128
                b_full = (OWN0 + i0 - 128) // 128
                for b in range(b_full + 1, b_hi + 1):
                    mask_idx[(s, b)] = mi
                    mi += 1
            assert mi == 11, mi
            for h in range(8):
                kt, rkt = kb[h % 2]
                vt, rvt = vb[h % 2]
                qt_, rqt = qb[h % 2]
                fw.dma(kt[:], KT[0][h], writes=[rkt])
                fw.dma(vt[:], VS[0].rearrange("(b p) f -> p b f", p=128)[:, :, 128 * h:128 * h + 128], writes=[rvt])
                fw.dma(qt_[:], QT[0][h], writes=[rqt])
                for s, (i0, n) in enumerate(QS):
                    b_hi = (OWN0 + i0 + n - 2) // 128
                    blocks = list(range(b_hi, -1, -1))
                    po, rpo = (pgen4, pgen5)[(h * 3 + s) % 2]
                    q_ap = qt_[:, i0:i0 + n]
                    st = {}

                    def stage1(b, k):
                        pz, rpz = next_ps()
                        mm(pz[:, 0:n], kt[:, 128 * b:128 * b + 128], q_ap, True, True, [rkt, rqt], [rpz])
                        e_, re_ = eb[k % 3]
                        actf(e_[:, 0:n], pz[:, 0:n], AF.Exp, [rpz], [re_])
                        sp_, rsp = spb[k % 4]
                        actf(sp_[:, 0:n], e_[:, 0:n], AF.Ln, [re_, rcst], [rsp], bias=cst[:, 1:2], scale=1.0)
                        if (s, b) in mask_idx:
                            m_ = sbm[:, mask_idx[(s, b)], 0:n]
                            spm, rspm = spmb[k % 4]
                            fw.op(pool, lambda e: e.tensor_tensor(spm[:, 0:n], sp_[:, 0:n], m_, op=ALU.mult), [rsp, rsbm], [rspm])
                            st[b] = (spm, rspm)
                        else:
                            st[b] = (sp_, rsp)

                    accs = {}
                    wms = {}

                    def stage2a(b, k):
                        spm, rspm = st[b]
                        first = (k == 0)
                        if k == 1:
                            accs[k] = st[blocks[0]]
                        elif k >= 2:
                            an, ran = accb[k % 3]
                            ap_, rap = accs[k - 1]
                            sprev, rsprev = st[blocks[k - 1]]
                            fw.op(pool, lambda e: e.tensor_tensor(an[:, 0:n], ap_[:, 0:n], sprev[:, 0:n], op=ALU.add), [rap, rsprev], [ran])
                            accs[k] = (an, ran)
                        pa, rpa = next_ps()
                        mm(pa[:, 0:n], kt[:, 128 * b:128 * b + 128], q_ap, True, False, [rkt, rqt], [rpa])
                        mm(pa[:, 0:n], Ltri, spm[:, 0:n], False, first, [rcm, rspm], [rpa])
                        if not first:
                            ap_, rap = accs[k]
                            mm(pa[:, 0:n], nones, ap_[:, 0:n], False, True, [rcm, rap], [rpa])
                        w_, rw_ = wb_[k % 3]
                        actf(w_[:, 0:n], pa[:, 0:n], AF.Exp, [rpa], [rw_])
                        if (s, b) in mask_idx:
                            m_ = sbm[:, mask_idx[(s, b)], 0:n]
                            wm, rwm = wmb[k % 3]
                            fw.op(pool, lambda e: e.tensor_tensor(wm[:, 0:n], w_[:, 0:n], m_, op=ALU.mult), [rw_, rsbm], [rwm])
                        else:
                            wm, rwm = w_, rw_
                        wms[k] = (wm, rwm)

                    def stage2b(b, k):
                        wm, rwm = wms.pop(k)
                        mm(po[:, 0:n], vt[:, b, :], wm[:, 0:n], k == 0, k == nb - 1, [rvt, rwm], [rpo])

                    nb = len(blocks)
                    stage1(blocks[0], 0)
                    pw, rpw = next_ps()
                    for _ in range(WARM_N):
                        mm(pw[:, 0:512], kt[:, 0:128], kt[:, 0:512], True, True, [rkt], [rpw])
                    if nb > 1:
                        stage1(blocks[1], 1)
                    stage2a(blocks[0], 0)
                    for k in range(nb):
                        if k + 2 < nb:
                            stage1(blocks[k + 2], k + 2)
                        if k + 1 < nb:
                            stage2a(blocks[k + 1], k + 1)
                        stage2b(blocks[k], k)
                        pump()
                    evac(o_sb[:, h, i0:i0 + n], po[:, 0:n], [rpo], [ro_sb])
            for _ in selg:
                pass
        fw.barrier()

        o_ds, ro_ds = sbt(LS, "o_ds", [128, 8, NQ], BF16)
        with scope(RS) as c3:
            ebt, rebt = sbt(c3, "ebt", [128, 8, TW], BF16)
            rbt, rrbt = sbt(c3, "rbt", [128, 8], F32)
            fw.dma(rbt, rb15, writes=[rrbt])
            for h in range(8):
                fw.dma(ebt[:, h, :], tbias[:, h, :], writes=[rebt], q="pool")
            selTb = [sbt(c3, f"selT{i}", [128, 32, 342], BF16) for i in range(2)]
            kb = [sbt(c3, f"dkb{i}", [128, 4096], BF16) for i in range(2)]
            vb = [sbt(c3, f"dvb{i}", [128, 32, 128], BF16) for i in range(2)]
            qb = [sbt(c3, f"dqb{i}", [128, 342], BF16) for i in range(2)]
            pmb = [sbt(c3, f"dpm{i}", [128, 342], BF16) for i in range(4)]
            osb = [sbt(c3, f"dos{i}", [128, 342], BF16) for i in range(2)]
            rdb = [sbt(c3, f"rden{i}", [128, 342], F32) for i in range(2)]
            o32b = [sbt(c3, f"o32{i}", [128, 342], F32) for i in range(2)]
            hcount = [0]

            def emit_head(s, h):
                i0, n, tau_min, nblk, nks = geom(s)
                selT_, rselT_ = selTb[s % 2]
                b_near = max(0, (tau_min - 255) // 128 + 1)
                hc = hcount[0]
                hcount[0] += 1
                kt, rkt = kb[hc % 2]
                vt, rvt = vb[hc % 2]
                qt_, rqt = qb[hc % 2]
                fw.dma(kt[:, 0:128 * nblk], KT[1][h, :, 0:128 * nblk], writes=[rkt])
                fw.dma(vt[:, 0:nblk, :], VS[1].rearrange("(b p) f -> p b f", p=128)[:, 0:nblk, 128 * h:128 * h + 128], writes=[rvt])
                fw.dma(qt_[:, 0:n], QT[1][h, :, i0:i0 + n], writes=[rqt])
                po, rpo = pO[hc % 2]
                pd, rpd = (pgen4, pgen5)[hc % 2]
                pls = {}
                pms = {}

                def sA(b):
                    pl, rpl = next_ps()
                    near = b >= b_near
                    mm(pl[:, 0:n], kt[:, 128 * b:128 * b + 128], qt_[:, 0:n], True, False, [rkt, rqt], [rpl])
                    mm(pl[:, 0:n], ident, selT_[:, b, 0:n], False, not near, [rcm, rselT_], [rpl])
                    if near:
                        delta = 128 * b - OWN0 - i0
                        jj = J0 - delta
                        assert 0 <= jj and jj + n <= TW, (jj, n)
                        mm(pl[:, 0:n], ident, ebt[:, h, jj:jj + n], False, True, [rcm, rebt], [rpl])
                    pls[b] = (pl, rpl)

                def sB(b):
                    pl, rpl = pls.pop(b)
                    pm, rpm = pmb[b % 4]
                    if b < b_near:
                        actf(pm[:, 0:n], pl[:, 0:n], AF.Exp, [rpl, rrbt], [rpm], bias=rbt[:, h:h + 1], scale=1.0)
                    else:
                        actf(pm[:, 0:n], pl[:, 0:n], AF.Exp, [rpl], [rpm])
                    pms[b] = (pm, rpm)

                def sC(b):
                    pm, rpm = pms.pop(b)
                    mm(po[:, 0:n], vt[:, b, :], pm[:, 0:n], b == 0, b == nblk - 1, [rvt, rpm], [rpo])
                    mm(pd[:, 0:n], ones, pm[:, 0:n], b == 0, b == nblk - 1, [rcm, rpm], [rpd])

                sA(0)
                if nblk > 1:
                    sA(1)
                sB(0)
                for b in range(nblk):
                    if b + 2 < nblk:
                        sA(b + 2)
                    if b + 1 < nblk:
                        sB(b + 1)
                    sC(b)
                rd_, rrd_ = rdb[hc % 2]
                o32, ro32 = o32b[hc % 2]
                actf(rd_[:, 0:n], pd[:, 0:n], AF.Ln, [rpd, rcst], [rrd_], bias=cst[:, 3:4], scale=1.0)
                actf(rd_[:, 0:n], rd_[:, 0:n], AF.Exp, [rrd_], [rrd_], scale=-1.0)
                actf(o32[:, 0:n], po[:, 0:n], AF.Copy, [rpo], [ro32])
                fw.op(pool, lambda e: e.tensor_tensor(o_ds[:, h, i0:i0 + n], o32[:, 0:n], rd_[:, 0:n], op=ALU.mult), [ro32, rrd_], [ro_ds])

            for s in range(3):
                _, n_, _, nblk_, _ = geom(s)
                st_, rst_ = selTb[s % 2]
                fw.dma(st_[:, 0:nblk_, 0:n_], SELS[s][:, 0:nblk_, 0:n_], reads=[rSEL[s]], writes=[rst_])
                for h in range(8):
                    emit_head(s, h)
        fw.barrier()
        print("phase3 min SBUF gap bytes/partition:", gapmin[0])

        h3, rh3 = sbt(RS, "h3", [128, KC, NQ], BF16)
        mark_h3 = RS.top
        mT, rmT = sbt(RS, "mT", [128, KC, NQ], BF16)
        with scope(RS) as c4a:
            hO, rhO = sbt(c4a, "hO", [128, KC, NQ], BF16)
            with scope(RS) as c4n:
                xs, rxs = sbt(c4n, "xs4", [128, KC, 342], F32)
                csq = mk_sq(c4n, 342)
                for (i0, n) in QS:
                    fw.dma(xs[:, :, 0:n], xw[:, :, OWN0 + i0: OWN0 + i0 + n], writes=[rxs])
                    rms_core(csq, xs, rxs, n, 0, lambda kc: hO[:, kc, i0:i0 + n], [rhO])
            fw.barrier()
            wg = [sbt(c4a, f"wg{i}", [128, KC, 256], BF16) for i in range(3)]
            wp = [sbt(c4a, f"wp{i}", [128, 8, 256], BF16) for i in range(3)]
            bg, rbg = sbt(c4a, "bg", [128, 32], F32)
            fw.dma(bg[:], bgate, writes=[rbg])
            gsb = [sbt(c4a, f"gs{i}", [128, 342], F32) for i in range(4)]
            tsb = [sbt(c4a, f"ts{i}", [128, 342], F32) for i in range(4)]
            for nch in range(16):
                wgt, rwg = wg[nch % 3]
                wpt, rwp = wp[nch % 3]
                load_w(wgt[:, :, 0:128], rwg, w_in[:, :, 7248 + 128 * nch: 7248 + 128 * nch + 128], KC)
                load_w(wgt[:, :, 128:256], rwg, w_in[:, :, 9296 + 128 * nch: 9296 + 128 * nch + 128], KC)
                load_w(wpt[:, :, 0:128], rwp, w_psb[:, :, 128 * nch:128 * nch + 128], 8)
                load_w(wpt[:, :, 128:256], rwp, w_pds[:, :, 128 * nch:128 * nch + 128], 8)
                for si, (i0, n) in enumerate(QS):
                    ts_ = []
                    for br in range(2):
                        pgt, rpg = next_ps()
                        for kc in range(KC):
                            mm(pgt[:, 0:n], wgt[:, kc, 128 * br:128 * br + 128], hO[:, kc, i0:i0 + n], kc == 0, kc == KC - 1, [rwg, rhO], [rpg])
                        g_, rg_ = gsb[(si * 2 + br) % 4]
                        actf(g_[:, 0:n], pgt[:, 0:n], AF.Sigmoid, [rpg, rbg], [rg_], bias=bg[:, 16 * br + nch: 16 * br + nch + 1], scale=1.0)
                        pp, rpp = next_ps()
                        osrc, rosrc = (o_sb, ro_sb) if br == 0 else (o_ds, ro_ds)
                        for kc in range(8):
                            mm(pp[:, 0:n], wpt[:, kc, 128 * br:128 * br + 128], osrc[:, kc, i0:i0 + n], kc == 0, kc == 7, [rwp, rosrc], [rpp])
                        t_, rt_ = tsb[(si * 2 + br) % 4]
                        fw.op(dve, lambda e: e.tensor_tensor(t_[:, 0:n], pp[:, 0:n], g_[:, 0:n], op=ALU.mult), [rpp, rg_], [rt_])
                        ts_.append((t_, rt_))
                    fw.op(dve, lambda e: e.tensor_tensor(mT[:, nch, i0:i0 + n], ts_[0][0][:, 0:n], ts_[1][0][:, 0:n], op=ALU.add),
                          [ts_[0][1], ts_[1][1]], [rmT])
        fw.barrier()
        LS.top = L_consts
        x1, rx1 = sbt(LS, "x1", [128, KC, NQ], F32)
        with scope(RS) as c4b:
            wo = [sbt(c4b, f"wo{i}", [128, KC, 128], BF16) for i in range(4)]
            xi = [sbt(c4b, f"xi{i}", [128, NQ], F32) for i in range(2)]
            for nch in range(16):
                wot, rwo = wo[nch % 4]
                load_w(wot, rwo, w_out[:, :, 128 * nch:128 * nch + 128], KC)
                xt, rxt = xi[nch % 2]
                fw.dma(xt[:, 0:NQ], xw[:, nch, OWN0:OWN0 + NQ], writes=[rxt])
                for (i0, n) in QS:
                    p, rp = next_ps()
                    for kc in range(KC):
                        mm(p[:, 0:n], wot[:, kc, :], mT[:, kc, i0:i0 + n], kc == 0, kc == KC - 1, [rwo, rmT], [rp])
                    fw.op(dve, lambda e: e.tensor_tensor(x1[:, nch, i0:i0 + n], p[:, 0:n], xt[:, i0:i0 + n], op=ALU.add), [rp, rxt], [rx1])
        fw.barrier()

        hq, rhq = mT, rmT
        with scope(RS) as c5:
            hm, rhm = sbt(c5, "hm", [128, KC, 256], BF16)
            qc, rqc = sbt(c5, "qc", [128, 4, NQ], BF16)
            km, rkm = sbt(c5, "km", [128, 4, 256], BF16)
            vm, rvm = sbt(c5, "vm", [128, 2, 512], BF16)
            oc, roc = sbt(c5, "oc", [128, 4, NQ], BF16)
            wt, rwt = sbt(c5, "w5", [128, KC, 512], BF16)
            load_w(wt, rwt, w_cq, KC)
            with scope(RS) as c5n:
                csq = mk_sq(c5n, 342)
                for (i0, n) in QS:
                    rms_core(csq, x1[:, :, i0:i0 + n], rx1, n, 1, lambda kc: hq[:, kc, i0:i0 + n], [rhq])
                xm, rxm = sbt(c5n, "xm", [128, KC, 256], F32)
                fw.dma(xm, memT, writes=[rxm])
                rms_core(csq, xm, rxm, 256, 2, lambda kc: hm[:, kc, :], [rhm])
            fw.barrier()
            wt2, rwt2 = sbt(c5, "w5b", [128, KC, 512], BF16)
            load_w(wt2, rwt2, w_ckv[:, :, 0:512], KC)
            for h in range(4):
                for (i0, n) in QS:
                    p, rp = next_ps()
                    for kc in range(KC):
                        mm(p[:, 0:n], wt[:, kc, 128 * h:128 * h + 128], hq[:, kc, i0:i0 + n], kc == 0, kc == KC - 1, [rwt, rhq], [rp])
                    evac(qc[:, h, i0:i0 + n], p[:, 0:n], [rp], [rqc], scale=128 ** -0.5)
            load_w(wt, rwt, w_ckv[:, :, 512:1024], KC)
            for h in range(4):
                p, rp = next_ps()
                for kc in range(KC):
                    mm(p[:, 0:256], wt2[:, kc, 128 * h:128 * h + 128], hm[:, kc, :], kc == 0, kc == KC - 1, [rwt2, rhm], [rp])
                evac(km[:, h, :], p[:, 0:256], [rp], [rkm])
            for mt in range(2):
                p, rp = next_ps()
                for kc in range(KC):
                    mm(p[:, 0:512], hm[:, kc, 128 * mt:128 * mt + 128], wt[:, kc, :], kc == 0, kc == KC - 1, [rwt, rhm], [rp])
                evac(vm[:, mt, :], p[:, 0:512], [rp], [rvm])
            pcb = [sbt(c5, f"pc{i}", [128, 342], BF16) for i in range(3)]
            rdc, rrdc = sbt(c5, "rdc", [128, 342], F32)
            kk = 0
            for h in range(4):
                for (i0, n) in QS:
                    po, rpo = pO[0]
                    pd, rpd = pO[1]
                    for mt in range(2):
                        pl, rpl = next_ps()
                        mm(pl[:, 0:n], km[:, h, 128 * mt:128 * mt + 128], qc[:, h, i0:i0 + n], True, True, [rkm, rqc], [rpl])
                        pc_, rpc = pcb[kk % 3]
                        kk += 1
                        actf(pc_[:, 0:n], pl[:, 0:n], AF.Exp, [rpl], [rpc])
                        mm(po[:, 0:n], vm[:, mt, 128 * h:128 * h + 128], pc_[:, 0:n], mt == 0, mt == 1, [rvm, rpc], [rpo])
                        mm(pd[:, 0:n], ones, pc_[:, 0:n], mt == 0, mt == 1, [rcm, rpc], [rpd])
                    fw.op(dve, lambda e: e.reciprocal(rdc[:, 0:n], pd[:, 0:n]), [rpd], [rrdc])
                    fw.op(dve, lambda e: e.tensor_tensor(oc[:, h, i0:i0 + n], po[:, 0:n], rdc[:, 0:n], op=ALU.mult), [rpo, rrdc], [roc])
            wc = [sbt(c5, f"wc{i}", [128, 4, 128], BF16) for i in range(2)]
            for nch in range(16):
                wct, rwc = wc[nch % 2]
                load_w(wct, rwc, w_co[:, :, 128 * nch:128 * nch + 128], 4)
                for (i0, n) in QS:
                    p, rp = next_ps()
                    for kc in range(4):
                        mm(p[:, 0:n], wct[:, kc, :], oc[:, kc, i0:i0 + n], kc == 0, kc == 3, [rwc, roc], [rp])
                    fw.op(dve, lambda e: e.tensor_tensor(x1[:, nch, i0:i0 + n], p[:, 0:n], x1[:, nch, i0:i0 + n], op=ALU.add), [rp, rx1], [rx1])
        fw.barrier()
        with scope(RS) as c6n:
            csq = mk_sq(c6n, 342)
            for (i0, n) in QS:
                rms_core(csq, x1[:, :, i0:i0 + n], rx1, n, 3, lambda kc: h3[:, kc, i0:i0 + n], [rh3])
        rx2s = Res()
        for kc in range(0, KC, 4):
            fw.dma(X2S[:, kc:kc + 4, :], x1[:, kc:kc + 4, :], reads=[rx1], writes=[rx2s])
        fw.barrier()
        RS.top = mark_h3
        LS.top = L_consts

        gat, rgat = sbt(LS, "gat", [128, 48, 1024], BF16)
        with scope(RS) as c6a:
            cw, rcw = sbt(c6a, "cw", [128, 3, 96], F32)
            cb, rcb = sbt(c6a, "cb", [128, 96], F32)
            fw.dma(cw, convw, writes=[rcw])
            fw.dma(cb, convb, writes=[rcb])
            wu = [sbt(c6a, f"wu{i}", [128, KC, 128], BF16) for i in range(6)]
            ub = [sbt(c6a, f"u{i}", [128, NQ], F32) for i in range(2)]
            cbuf = [sbt(c6a, f"c{i}", [128, 1024], F32) for i in range(2)]
            ga, rga = sbt(c6a, "ga", [128, 1024], F32)
            wk = 0
            for i in range(48):
                cs = []
                for part in range(2):
                    ch = i + 48 * part
                    wt, rwt = wu[wk % 6]
                    wk += 1
                    load_w(wt, rwt, w_up[:, :, 128 * ch:128 * ch + 128], KC)
                    u_, ru = ub[part]
                    for (i0, n) in QS:
                        p, rp = next_ps()
                        for kc in range(KC):
                            mm(p[:, 0:n], wt[:, kc, :], h3[:, kc, i0:i0 + n], kc == 0, kc == KC - 1, [rwt, rh3], [rp])
                        actf(u_[:, i0:i0 + n], p[:, 0:n], AF.Copy, [rp], [ru])
                    fw.op(dve, lambda e: e.tensor_scalar(u_[:, 0:2], u_[:, 0:2], hvt[:, 0:1], None, op0=ALU.mult), [ru, rhv], [ru])
                    c_, rc_ = cbuf[part]
                    fw.op(dve, lambda e: e.tensor_scalar(c_[:, 0:1024], u_[:, 2:1026], cw[:, 2, ch:ch + 1], cb[:, ch:ch + 1],
                                                         op0=ALU.mult, op1=ALU.add), [ru, rcw, rcb], [rc_])
                    fw.op(dve, lambda e: e.scalar_tensor_tensor(c_[:, 0:1024], u_[:, 1:1025], cw[:, 1, ch:ch + 1], c_[:, 0:1024],
                                                                op0=ALU.mult, op1=ALU.add), [ru, rcw, rc_], [rc_])
                    fw.op(dve, lambda e: e.scalar_tensor_tensor(c_[:, 0:1024], u_[:, 0:1024], cw[:, 0, ch:ch + 1], c_[:, 0:1024],
                                                                op0=ALU.mult, op1=ALU.add), [ru, rcw, rc_], [rc_])
                    cs.append((c_, rc_))
                actf(ga[:, 0:1024], cs[0][0][:, 0:1024], AF.Gelu_apprx_tanh, [cs[0][1]], [rga])
                fw.op(dve, lambda e: e.tensor_tensor(gat[:, i, :], ga[:, 0:1024], cs[1][0][:, 0:1024], op=ALU.mult), [rga, cs[1][1]], [rgat])
        fw.barrier()
        RS.top = ARENA * 4
        with scope(RS) as c7:
            wd = [sbt(c7, f"wd{i}", [128, 48, 128], BF16) for i in range(3)]
            xi = [sbt(c7, f"x2i{i}", [128, 1024], F32) for i in range(2)]
            x3c = [sbt(c7, f"x3c{i}", [128, 1024], F32) for i in range(3)]
            sqc = [sbt(c7, f"sqc{i}", [128, 512], BF16) for i in range(3)]
            lnf, rlnf = sbt(c7, "lnf", [128, 1024], F32)
            rsf, rrsf = sbt(c7, "rsf", [128, 1024], F32)
            ost = [sbt(c7, f"ost{i}", [128, 1024], F32) for i in range(2)]
            pss = [pgen4, pgen5]
            rx3s = [Res() for _ in range(16)]
            sqi = 0
            for nch in range(16):
                wdt, rwd = wd[nch % 3]
                for a in range(0, 48, 12):
                    fw.dma(wdt[:, a:a + 12, :], w_down[:, a:a + 12, 128 * nch:128 * nch + 128], writes=[rwd], q="pool")
                xt, rxt = xi[nch % 2]
                fw.dma(xt, X2S[:, nch, 2:1026], reads=[rx2s], writes=[rxt])
                x3_, rx3_ = x3c[nch % 3]
                for sl in range(2):
                    a0 = 512 * sl
                    p, rp = next_ps()
                    for kc in range(48):
                        mm(p[:, 0:512], wdt[:, kc, :], gat[:, kc, a0:a0 + 512], kc == 0, kc == 47, [rwd, rgat], [rp])
                    fw.op(dve, lambda e: e.tensor_tensor(x3_[:, a0:a0 + 512], p[:, 0:512], xt[:, a0:a0 + 512], op=ALU.add), [rp, rxt], [rx3_])
                    sq_, rsq_ = sqc[sqi % 3]
                    sqi += 1
                    actf(sq_, x3_[:, a0:a0 + 512], AF.Square, [rx3_], [rsq_])
                    ps_, rps_ = pss[sl]
                    mm(ps_[:, 0:512], ones, sq_, nch == 0, nch == 15, [rsq_, rcm], [rps_])
                fw.dma(X3S[:, nch, :], x3_, reads=[rx3_], writes=[rx3s[nch]], q="act")
            for sl in range(2):
                ps_, rps_ = pss[sl]
                actf(lnf[:, 512 * sl:512 * sl + 512], ps_[:, 0:512], AF.Ln, [rps_, rcst], [rlnf], bias=cst[:, 0:1], scale=1.0 / D)
            actf(rsf, lnf, AF.Exp, [rlnf], [rrsf], scale=-0.5)
            for nch in range(16):
                x3_, rx3_ = x3c[nch % 3]
                fw.dma(x3_, X3S[:, nch, :], reads=[rx3s[nch]], writes=[rx3_])
                o_, ro_ = ost[nch % 2]
                fw.op(dve, lambda e: e.scalar_tensor_tensor(o_, x3_, gv[:, 4, nch:nch + 1], rsf, op0=ALU.mult, op1=ALU.mult),
                      [rx3_, rrsf, rgv], [ro_])
                fw.dma(outT[:, nch, :], o_, reads=[ro_], q="pool")
        fw.finish()
        print("instructions emitted:", fw.n_inst)
    return nc


def _bucket(rel):
    rel = np.asarray(rel, np.int64)
    nb = 16
    max_exact = 8
    ret = np.where(rel > 0, nb, 0)
    n = np.abs(rel)
    nf = np.maximum(n, 1).astype(np.float32)
    lg = (np.log(nf / np.float32(max_exact)).astype(np.float32) / np.float32(np.log(128 / 8))).astype(np.float32)
    large = max_exact + (lg * np.float32(nb - max_exact)).astype(np.int32)
    large = np.minimum(large, nb - 1)
    return ret + np.where(n < max_exact, n, large)


def _fm(v, n):
    return np.ascontiguousarray(np.asarray(v, np.float32).reshape(n, 128).T)


_NC_CACHE = {}


def kernel(x, mem, g_mix, w_in, b_gate, w_proj_sb, w_proj_dsa, w_out, rel_bias, g_cross, g_mem,
           w_cq, w_ckv, w_co, g_ffn, w_up, conv_w, conv_b, w_down, g_final):
    f32 = np.float32
    x = np.asarray(x, f32)
    mem = np.asarray(mem, f32)
    rel_bias = np.asarray(rel_bias, f32)
    if "nc" not in _NC_CACHE:
        _NC_CACHE["nc"] = build()
    nc = _NC_CACHE["nc"]

    shared = {
        "w_in": np.ascontiguousarray(np.asarray(w_in, f32)[0]),
        "w_proj_sb": np.ascontiguousarray(np.asarray(w_proj_sb, f32)[0]),
        "w_proj_dsa": np.ascontiguousarray(np.asarray(w_proj_dsa, f32)[0]),
        "w_out": np.ascontiguousarray(np.asarray(w_out, f32)[0]),
        "w_cq": np.ascontiguousarray(np.asarray(w_cq, f32)[0]),
        "w_ckv": np.ascontiguousarray(np.asarray(w_ckv, f32)[0]),
        "w_co": np.ascontiguousarray(np.asarray(w_co, f32)[0]),
        "w_up": np.ascontiguousarray(np.asarray(w_up, f32)[0]),
        "w_down": np.ascontiguousarray(np.asarray(w_down, f32)[0]),
    }
    gs = np.stack([_fm(np.asarray(g, f32).reshape(-1), 16) for g in (g_mix, g_cross, g_mem, g_ffn, g_final)], axis=1)
    shared["gvec"] = np.ascontiguousarray(gs)
    shared["bgate"] = _fm(np.asarray(b_gate, f32).reshape(-1), 32)
    cwv = np.asarray(conv_w, f32)[0]
    shared["convw"] = np.ascontiguousarray(np.stack([_fm(cwv[i], 96) for i in range(3)], axis=1))
    shared["convb"] = _fm(np.asarray(conv_b, f32).reshape(-1), 96)
    p = np.arange(128)[:, None]
    jj = np.arange(TW)[None, :]
    bk = _bucket(p - (jj - J0))
    shared["tbias"] = np.ascontiguousarray(np.transpose(rel_bias[bk], (0, 2, 1)))
    shared["rb15"] = np.ascontiguousarray(np.broadcast_to(rel_bias[15][None, :], (128, 8)))
    sbm = np.zeros((128, 11, 342), f32)
    mi = 0
    for s, (i0, n) in enumerate(QS):
        b_hi = (OWN0 + i0 + n - 2) // 128
        b_full = (OWN0 + i0 - 128) // 128
        for b in range(b_full + 1, b_hi + 1):
            ks = 128 * b + np.arange(128)[:, None]
            qs = OWN0 + i0 + np.arange(342)[None, :]
            sbm[:, mi, :] = (ks < qs).astype(f32)
            mi += 1
    shared["sbm"] = sbm
    cm = np.zeros((128, 4, 128), f32)
    jx = np.arange(128)[:, None]
    sx = np.arange(128)[None, :]
    cm[:, 0, :] = -(jx >= sx).astype(f32)
    cm[:, 1, :] = -1.0
    cm[:, 2, :] = 1.0
    cm[:, 3, :] = np.eye(128, dtype=f32)
    shared["cmat"] = cm
    shared["ckv"] = np.ascontiguousarray(np.broadcast_to((2.0 ** -(np.arange(NIT) + 1.0)).astype(f32)[None, :], (128, NIT)))

    in_maps = []
    for c in range(8):
        b, j = c // 4, c % 4
        t0 = j * 1024
        win = np.zeros((4096, D), f32)
        lo = t0 - 3072
        src0 = max(lo, 0)
        win[src0 - lo:, :] = x[b, src0:t0 + 1024, :]
        xwin = np.ascontiguousarray(win.T.reshape(16, 128, 4096).transpose(1, 0, 2))
        mT = np.ascontiguousarray(mem[b].T.reshape(16, 128, 256).transpose(1, 0, 2))
        tq = t0 - 2 + np.arange(NQ)[:, None]
        tk = lo + np.arange(4096)[None, :]
        vis = (tk >= 0) & ((tk // 64) <= (tq // 64))
        dm = np.where(vis, 0.0, -1e30).astype(f32)
        m = dict(shared)
        m["xw"] = xwin
        m["memT"] = mT
        m["dmask"] = np.ascontiguousarray(dm)
        m["hv"] = np.full((128, 1), 1.0 if j > 0 else 0.0, f32)
        in_maps.append(m)

    res = run_bass_kernel_spmd(nc, in_maps, core_ids=list(range(8)))
    out = np.zeros((2, 4096, D), f32)
    for c in range(8):
        b, j = c // 4, c % 4
        o = np.asarray(res.results[c]["outT"], f32)
        out[b, j * 1024:(j + 1) * 1024, :] = o.transpose(2, 1, 0).reshape(1024, D)
    if DEBUG:
        kernel.last = res
    return out
```
